# Optimizing a Trainium2 kernel written in Bass

```python
import math
import jax, jax.numpy as jnp
from jax import lax
import numpy as np

D_MODEL = 1024
BATCH = 8
SEQ = 8192
DEPTH = 2

CHUNK = 64
LEFT_CHUNKS = 8
BAND = (LEFT_CHUNKS + 1) * CHUNK
HEAD_DIM = 64
A_HEADS = 8
B_HEADS = 4
C_HEADS = 8
A_WIDTH = A_HEADS * HEAD_DIM
B_WIDTH = B_HEADS * 2 * HEAD_DIM
C_WIDTH = C_HEADS * HEAD_DIM
IN_WIDTH = 3 * A_WIDTH + 3 * B_WIDTH + 3 * C_WIDTH + C_HEADS
N_BRANCH = 3
D_FF = 4 * D_MODEL
REL_CLIP = 128
ROPE_THETA = 500000.0
ROPE_DIM = HEAD_DIM // 4
Q_BLOCK = 128
EPS = 1e-6

kernel_name = "streaming_hybrid_gated_attention_trunk"


def rmsnorm(x, g):
    xf = x.astype(jnp.float32)
    y = xf * lax.rsqrt(jnp.mean(xf * xf, axis=-1, keepdims=True) + EPS)
    return (y * g.astype(jnp.float32)).astype(x.dtype)


def partial_rope(x, positions):
    half = ROPE_DIM // 2
    inv_freq = jnp.power(ROPE_THETA, -jnp.arange(0, ROPE_DIM, 2, dtype=jnp.float32) / ROPE_DIM)
    ang = positions.astype(jnp.float32)[..., None] * inv_freq
    ang = ang.reshape(ang.shape[:2] + (1,) * (x.ndim - 3) + (half,))
    cos, sin = jnp.cos(ang), jnp.sin(ang)
    xf = x.astype(jnp.float32)
    x1, x2, rest = xf[..., :half], xf[..., half:ROPE_DIM], xf[..., ROPE_DIM:]
    out = jnp.concatenate([x1 * cos - x2 * sin, x2 * cos + x1 * sin, rest], axis=-1)
    return out.astype(x.dtype)


def chunked_relbias_attention(q, k, v, rel_bias):
    B, S, H, dh = q.shape
    nc = S // CHUNK
    pad = LEFT_CHUNKS * CHUNK
    k_pad = jnp.pad(k, ((0, 0), (pad, 0), (0, 0), (0, 0)))
    v_pad = jnp.pad(v, ((0, 0), (pad, 0), (0, 0), (0, 0)))
    qc = q.reshape(B, nc, CHUNK, H, dh).transpose(1, 0, 2, 3, 4)
    q_off = jnp.arange(CHUNK)
    k_off = jnp.arange(BAND) - pad
    rel = jnp.clip(k_off[None, :] - q_off[:, None], -REL_CLIP, REL_CLIP) + REL_CLIP
    bias = rel_bias[:, rel].astype(jnp.float32)
    scale = dh ** -0.5

    def one_chunk(args):
        c, qb = args
        kb = lax.dynamic_slice_in_dim(k_pad, c * CHUNK, BAND, axis=1)
        vb = lax.dynamic_slice_in_dim(v_pad, c * CHUNK, BAND, axis=1)
        s = jnp.einsum('bqhd,bkhd->bhqk', qb, kb, preferred_element_type=jnp.float32) * scale + bias
        valid = (c * CHUNK + k_off) >= 0
        s = jnp.where(valid[None, None, None, :], s, -jnp.inf)
        p = jax.nn.softmax(s, axis=-1)
        return jnp.einsum('bhqk,bkhd->bqhd', p.astype(vb.dtype), vb)

    out = lax.map(one_chunk, (jnp.arange(nc), qc))
    return out.transpose(1, 0, 2, 3, 4).reshape(B, S, H * dh)


def differential_attention(q, k, v, lambdas, sub_gain, lam_init):
    B, S, H, _, dh = q.shape
    nb = S // Q_BLOCK
    scale = dh ** -0.5
    lf = lambdas.astype(jnp.float32)
    lam = jnp.exp(jnp.sum(lf[0] * lf[1])) - jnp.exp(jnp.sum(lf[2] * lf[3])) + lam_init
    qb = q.reshape(B, nb, Q_BLOCK, H, 2, dh).transpose(1, 0, 2, 3, 4, 5)
    key_chunk = jnp.arange(S) // CHUNK

    def one_block(args):
        i, qi = args
        q_chunk = (i * Q_BLOCK + jnp.arange(Q_BLOCK)) // CHUNK
        mask = key_chunk[None, :] <= q_chunk[:, None]
        s = jnp.einsum('bqhcd,bkhcd->bhcqk', qi, k, preferred_element_type=jnp.float32) * scale
        s = jnp.where(mask, s, -jnp.inf)
        p = jax.nn.softmax(s, axis=-1)
        pd = p[:, :, 0] - lam * p[:, :, 1]
        return jnp.einsum('bhqk,bkhe->bqhe', pd.astype(v.dtype), v)

    o = lax.map(one_block, (jnp.arange(nb), qb))
    o = o.transpose(1, 0, 2, 3, 4).reshape(B, S, H, 2 * dh)
    o = rmsnorm(o, sub_gain) * (1.0 - lam_init)
    return o.reshape(B, S, H * 2 * dh)


def forgetting_attention(q, k, v, f_logit):
    B, S, H, dh = q.shape
    nb = S // Q_BLOCK
    scale = dh ** -0.5
    F = jnp.cumsum(jax.nn.log_sigmoid(f_logit.astype(jnp.float32)), axis=1)
    F_k = F.transpose(0, 2, 1)
    qb = q.reshape(B, nb, Q_BLOCK, H, dh).transpose(1, 0, 2, 3, 4)
    Fq = F.reshape(B, nb, Q_BLOCK, H).transpose(1, 0, 3, 2)
    k_pos = jnp.arange(S)

    def one_block(args):
        i, qi, fq = args
        q_pos = i * Q_BLOCK + jnp.arange(Q_BLOCK)
        mask = k_pos[None, :] <= q_pos[:, None]
        s = jnp.einsum('bqhd,bkhd->bhqk', qi, k, preferred_element_type=jnp.float32) * scale
        s = s + (fq[..., None] - F_k[:, :, None, :])
        s = jnp.where(mask, s, -jnp.inf)
        p = jax.nn.softmax(s, axis=-1)
        return jnp.einsum('bhqk,bkhd->bqhd', p.astype(v.dtype), v)

    o = lax.map(one_block, (jnp.arange(nb), qb, Fq))
    return o.transpose(1, 0, 2, 3, 4).reshape(B, S, H * dh)


def split_columns(proj):
    sizes = (A_WIDTH,) * 3 + (B_WIDTH,) * 3 + (C_WIDTH,) * 3 + (C_HEADS,)
    outs = []
    start = 0
    for n in sizes:
        outs.append(proj[..., start:start + n])
        start += n
    return outs


def hybrid_layer(x, positions, norm_mix, w_in, rel_bias, lambdas, sub_gain, b_forget,
                 w_br_a, w_br_b, w_br_c, w_gate, b_gate, w_out, norm_mlp, w_ff1, w_ff2, layer_idx):
    B, S, D = x.shape
    h = rmsnorm(x, norm_mix)
    proj = h @ w_in
    qa, ka, va, qb, kb, vb, qc, kc, vc, fl = split_columns(proj)
    ya = chunked_relbias_attention(qa.reshape(B, S, A_HEADS, HEAD_DIM), ka.reshape(B, S, A_HEADS, HEAD_DIM),
                                   va.reshape(B, S, A_HEADS, HEAD_DIM), rel_bias)
    lam_init = 0.8 - 0.6 * math.exp(-0.3 * layer_idx)
    qb = partial_rope(qb.reshape(B, S, B_HEADS, 2, HEAD_DIM), positions)
    kb = partial_rope(kb.reshape(B, S, B_HEADS, 2, HEAD_DIM), positions)
    yb = differential_attention(qb, kb, vb.reshape(B, S, B_HEADS, 2 * HEAD_DIM), lambdas, sub_gain, lam_init)
    yc = forgetting_attention(qc.reshape(B, S, C_HEADS, HEAD_DIM), kc.reshape(B, S, C_HEADS, HEAD_DIM),
                              vc.reshape(B, S, C_HEADS, HEAD_DIM), fl + b_forget)
    g = jax.nn.sigmoid((h @ w_gate + b_gate).astype(jnp.float32)).astype(x.dtype).reshape(B, S, N_BRANCH, D)
    merged = g[:, :, 0] * (ya @ w_br_a) + g[:, :, 1] * (yb @ w_br_b) + g[:, :, 2] * (yc @ w_br_c)
    x = x + merged @ w_out
    h2 = rmsnorm(x, norm_mlp)
    x = x + jnp.square(jax.nn.relu(h2 @ w_ff1)) @ w_ff2
    return x


def setup_inputs(seed: int = 0) -> dict:
    key = jax.random.key(seed)
    ks = jax.random.split(key, 20)
    f32 = jnp.float32
    nrm = lambda k, shape, s: jax.random.normal(k, shape, f32) * s
    x = nrm(ks[0], (BATCH, SEQ, D_MODEL), 1.0)
    offset = jax.random.randint(ks[1], (BATCH, 1), 0, 64, dtype=jnp.int32) * CHUNK
    positions = (offset + jnp.arange(SEQ, dtype=jnp.int32)[None, :]).astype(jnp.int32)
    return {
        "x": x,
        "positions": positions,
        "norm_mix": 1.0 + nrm(ks[2], (DEPTH, D_MODEL), 0.02),
        "w_in": nrm(ks[3], (DEPTH, D_MODEL, IN_WIDTH), D_MODEL ** -0.5),
        "rel_bias": nrm(ks[4], (DEPTH, A_HEADS, 2 * REL_CLIP + 1), 0.5),
        "lambdas": nrm(ks[5], (DEPTH, 4, HEAD_DIM), 0.1),
        "sub_gain": 1.0 + nrm(ks[6], (DEPTH, 2 * HEAD_DIM), 0.02),
        "b_forget": 2.0 + nrm(ks[7], (DEPTH, C_HEADS), 0.5),
        "w_br_a": nrm(ks[8], (DEPTH, A_WIDTH, D_MODEL), A_WIDTH ** -0.5),
        "w_br_b": nrm(ks[9], (DEPTH, B_WIDTH, D_MODEL), B_WIDTH ** -0.5),
        "w_br_c": nrm(ks[10], (DEPTH, C_WIDTH, D_MODEL), C_WIDTH ** -0.5),
        "w_gate": nrm(ks[11], (DEPTH, D_MODEL, N_BRANCH * D_MODEL), D_MODEL ** -0.5),
        "b_gate": nrm(ks[12], (DEPTH, N_BRANCH * D_MODEL), 0.02),
        "w_out": nrm(ks[13], (DEPTH, D_MODEL, D_MODEL), D_MODEL ** -0.5),
        "norm_mlp": 1.0 + nrm(ks[14], (DEPTH, D_MODEL), 0.02),
        "w_ff1": nrm(ks[15], (DEPTH, D_MODEL, D_FF), D_MODEL ** -0.5),
        "w_ff2": nrm(ks[16], (DEPTH, D_FF, D_MODEL), D_FF ** -0.5),
        "final_norm": 1.0 + nrm(ks[17], (D_MODEL,), 0.02),
    }


def reference(x, positions, norm_mix, w_in, rel_bias, lambdas, sub_gain, b_forget,
              w_br_a, w_br_b, w_br_c, w_gate, b_gate, w_out, norm_mlp, w_ff1, w_ff2, final_norm):
    for l in range(DEPTH):
        x = hybrid_layer(x, positions, norm_mix[l], w_in[l], rel_bias[l], lambdas[l], sub_gain[l], b_forget[l],
                         w_br_a[l], w_br_b[l], w_br_c[l], w_gate[l], b_gate[l], w_out[l],
                         norm_mlp[l], w_ff1[l], w_ff2[l], l)
    return rmsnorm(x, final_norm)
```

```python
import math
from contextlib import ExitStack

import numpy as np
import concourse.bass as bass
import concourse.mybir as mybir
from concourse.bass_utils import run_bass_kernel_spmd

F32 = mybir.dt.float32
BF16 = mybir.dt.bfloat16
I32 = mybir.dt.int32
AF = mybir.ActivationFunctionType
ALU = mybir.AluOpType

D = 1024
DEPTH = 2
NCH = D // 128
INW = 4616
DFF = 4096
EPS = 1e-6
NEG = -1.0e5

ENGS = ("pe", "act", "dve", "pool", "sp")
SP_KEYS = ("const", "g", "bf", "q0", "q1", "k0", "k1", "kk0", "kk1")
SAME_ENGINE_SYNC = True


class Tile:
    __slots__ = ("name", "ap", "w", "r")

    def __init__(self, name, ap):
        self.name = name
        self.ap = ap
        self.w = None
        self.r = []


class Op:
    __slots__ = ("eng", "fn", "deps", "signal", "semval", "is_dma", "sem", "name")

    def __init__(self, eng, fn, name=""):
        self.eng = eng
        self.fn = fn
        self.deps = []
        self.signal = False
        self.semval = 0
        self.is_dma = False
        self.sem = None
        self.name = name


class Prog:
    def __init__(self, nc):
        self.nc = nc
        self.ops = {e: [] for e in ENGS}
        self.dma_keys = {}
        self.tiles = []
        self.n_ops = 0

    def tile(self, name, ap):
        t = Tile(name, ap)
        self.tiles.append(t)
        return t

    def _add_deps(self, op, reads, writes):
        deps = op.deps
        for t in reads:
            if t.w is not None and t.w not in deps:
                deps.append(t.w)
        for t in writes:
            if t.w is not None and t.w not in deps:
                deps.append(t.w)
            for r in t.r:
                if r not in deps and r is not op:
                    deps.append(r)
        for t in reads:
            t.r.append(op)
        for t in writes:
            t.w = op
            t.r = []

    def op(self, eng, fn, reads=(), writes=(), name=""):
        o = Op(eng, fn, name)
        self._add_deps(o, reads, writes)
        self.ops[eng].append(o)
        self.n_ops += 1
        return o

    def dma(self, eng, fn, n, key, reads=(), writes=(), name=""):
        eng = "sp" if key in SP_KEYS else "pool"
        o = Op(eng, fn, name)
        o.is_dma = True
        st = self.dma_keys.setdefault(key, [0, None])
        if st[1] is not None:
            o.deps.append(st[1])
        st[0] += n
        st[1] = o
        o.sem = key
        o.semval = 16 * st[0]
        self._add_deps(o, reads, writes)
        self.ops[eng].append(o)
        self.n_ops += 1
        return o

    def barrier(self):
        lasts = []
        for e in ENGS:
            for o in reversed(self.ops[e]):
                if o.fn is not None and not o.is_dma:
                    lasts.append(o)
                    break
        for k, st in self.dma_keys.items():
            if st[1] is not None:
                lasts.append(st[1])
        for e in ENGS:
            o = Op(e, None, "barrier")
            o.deps = [l for l in lasts if (l.is_dma or l.eng != e)]
            self.ops[e].append(o)
        for t in self.tiles:
            t.w = None
            t.r = []

    def emit(self):
        nc = self.nc
        for e in ENGS:
            for o in self.ops[e]:
                for d in o.deps:
                    if not d.is_dma:
                        if d.eng == o.eng and not SAME_ENGINE_SYNC:
                            continue
                        d.signal = True
        for e in ENGS:
            c = 0
            for o in self.ops[e]:
                if o.is_dma or o.fn is None:
                    continue
                if o.signal:
                    c += 1
                    o.semval = c
        with ExitStack() as es:
            esem = {e: es.enter_context(nc.semaphore("s_" + e)) for e in ENGS}
            dsem = {k: es.enter_context(nc.semaphore("d_%s" % (k,))) for k in self.dma_keys}
            block = es.enter_context(nc.Block())

            def run(e, eng):
                known = {}
                for o in self.ops[e]:
                    for d in o.deps:
                        if d.is_dma:
                            sem = dsem[d.sem]
                        else:
                            if d.eng == e and not SAME_ENGINE_SYNC:
                                continue
                            sem = esem[d.eng]
                        if known.get(sem.num, 0) >= d.semval:
                            continue
                        known[sem.num] = d.semval
                        eng.wait_ge(sem, d.semval)
                    if o.fn is None:
                        continue
                    if o.is_dma:
                        o.fn(eng, dsem[o.sem])
                    else:
                        ins = o.fn(eng)
                        if o.signal:
                            ins.then_inc(esem[e], 1)

            @block.tensor
            def _(eng):
                run("pe", eng)

            @block.scalar
            def _(eng):
                run("act", eng)

            @block.vector
            def _(eng):
                run("dve", eng)

            @block.gpsimd
            def _(eng):
                run("pool", eng)

            @block.sync
            def _(eng):
                run("sp", eng)


class Arena:
    def __init__(self, nc, nbytes):
        self.t = nc.alloc_sbuf_tensor("arena", [128, nbytes // 4], F32)
        self.cap = nbytes
        self.off = 0

    def mark(self):
        return self.off

    def reset(self, m=0):
        self.off = m

    def alloc(self, shape, dtype):
        esz = 2 if dtype == BF16 else 4
        n = 1
        for s in shape:
            n *= s
        nb = (n * esz + 31) // 32 * 32
        assert self.off + nb <= self.cap, ("SBUF arena overflow", self.off, nb, self.cap)
        a = self.t[:, self.off // 4:(self.off + nb) // 4]
        self.off += nb
        if dtype != F32:
            a = a.bitcast(dtype)
        a = a[:, 0:n]
        if len(shape) == 2:
            a = a.rearrange("p (a b) -> p a b", a=shape[0])
        elif len(shape) == 3:
            a = a.rearrange("p (a b c) -> p a b c", a=shape[0], b=shape[1])
        return a


def build_program(S, debug=False, n_layers=DEPTH, stop_after=None):
    assert S % 512 == 0
    NB = S // 512
    NT = S // 128
    nc = bass.Bass("TRN2", target_bir_lowering=False)
    P = Prog(nc)
    ext_out = "ExternalOutput" if debug else "Internal"

    def din(name, shape, dt=F32):
        return nc.dram_tensor(name, list(shape), dt, kind="ExternalInput").ap()

    x_in = din("x", [S, D])
    pos_in = din("positions", [1, S], I32)
    norm_mix = din("norm_mix", [DEPTH, D])
    w_in = din("w_in", [DEPTH, D, INW])
    biasA = din("biasA", [DEPTH, 5, 128, 8, 128])
    lambdas = din("lambdas", [DEPTH, 1, 256])
    sub_gain = din("sub_gain", [DEPTH, 128])
    b_forget = din("b_forget", [DEPTH, 8])
    w_br = [din("w_br_a", [DEPTH, 512, D]), din("w_br_b", [DEPTH, 512, D]), din("w_br_c", [DEPTH, 512, D])]
    w_gate = din("w_gate", [DEPTH, D, 3 * D])
    b_gate = din("b_gate", [DEPTH, 3 * D])
    w_out = din("w_out", [DEPTH, D, D])
    norm_mlp = din("norm_mlp", [DEPTH, D])
    w_ff1 = din("w_ff1", [DEPTH, D, DFF])
    w_ff2 = din("w_ff2", [DEPTH, DFF, D])
    final_norm = din("final_norm", [D])
    c_ident = din("c_ident", [128, 128])
    c_rope = din("c_rope", [128, 2])
    c_maskC = din("c_maskC", [4, 128, 512])
    c_maskB = din("c_maskB", [4, 128, 512])
    c_BT = din("c_BT", [128, 128])

    out = nc.dram_tensor("out", [S, D], F32, kind="ExternalOutput").ap()

    def scr(name, shape, dt, dbg=True):
        return nc.dram_tensor(name, list(shape), dt, kind=(ext_out if dbg else "Internal")).ap()

    xT = scr("xT", [NCH, 128, S], F32)
    ropeC = scr("ropeC", [128, S], F32)
    ropeS = scr("ropeS", [128, S], F32)
    QT = {m: scr("QT" + m, [512, S], BF16) for m in "abc"}
    KT = {m: scr("KT" + m, [512, S], BF16) for m in "abc"}
    V = {m: scr("V" + m, [S, 512], BF16) for m in "abc"}
    ls_scr = scr("ls", [8, S], F32)
    Gq = scr("Gq", [8, S], BF16)
    yT = scr("yT", [1536, S], BF16)

    T_xT = P.tile("xT", xT)
    T_rope = P.tile("rope", ropeC)
    T_qkv = P.tile("qkv", None)
    T_ls = P.tile("ls", None)
    T_Gq = P.tile("Gq", None)
    T_yT = P.tile("yT", None)
    T_out = P.tile("out", None)

    arena = Arena(nc, 207 * 1024)
    psall = nc.alloc_psum_tensor("psall", [128, 4096], F32)
    ps = [P.tile("ps%d" % i, psall[:, i * 512:(i + 1) * 512]) for i in range(8)]

    def sb(name, shape, dt):
        return P.tile(name, arena.alloc(shape, dt))

    ident = sb("ident", [128], F32)
    ones_bf = sb("ones_bf", [128], BF16)
    ones_f = sb("ones_f", [128], F32)
    ropec = sb("ropec", [2], F32)
    P.dma("sp", lambda e, s: e.dma_start(out=ident.ap, in_=c_ident).then_inc(s, 16), 1, "const", writes=[ident])
    P.dma("sp", lambda e, s: e.dma_start(out=ropec.ap, in_=c_rope).then_inc(s, 16), 1, "const", writes=[ropec])
    P.op("dve", lambda e: e.memset(ones_bf.ap, 1.0), writes=[ones_bf])
    P.op("dve", lambda e: e.memset(ones_f.ap, 1.0), writes=[ones_f])
    persist_mark = arena.mark()

    def phase_pre():
        xb = [sb("pre_x%d" % i, [4, D], F32) for i in range(2)]
        xt = [sb("pre_xt%d" % i, [NCH, 512], F32) for i in range(2)]
        for j in range(NB):
            xi, xo = xb[j % 2], xt[j % 2]
            src = x_in[j * 512:(j + 1) * 512, :].rearrange("(t p) d -> p t d", p=128)
            P.dma("sp", lambda e, s, xi=xi, src=src: e.dma_start(out=xi.ap, in_=src).then_inc(s, 16), 1,
                  "pre_x%d" % (j % 2), writes=[xi])
            for c in range(NCH):
                pt = ps[c % 4]

                def f(e, xi=xi, pt=pt, c=c):
                    for tt in range(4):
                        ins = e.transpose(out=pt.ap[:, tt * 128:(tt + 1) * 128],
                                          in_=xi.ap[:, tt, c * 128:(c + 1) * 128], identity=ident.ap)
                    return ins
                P.op("pe", f, reads=[xi, ident], writes=[pt])
                if c % 2 == 0:
                    P.op("dve", lambda e, xo=xo, pt=pt, c=c: e.tensor_copy(out=xo.ap[:, c, :], in_=pt.ap),
                         reads=[pt], writes=[xo])
                else:
                    P.op("act", lambda e, xo=xo, pt=pt, c=c: e.activation(out=xo.ap[:, c, :], in_=pt.ap, func=AF.Copy),
                         reads=[pt], writes=[xo])
            dst = xT[:, :, j * 512:(j + 1) * 512].rearrange("c p t -> p c t")
            P.dma("pool", lambda e, s, xo=xo, dst=dst: e.dma_start(out=dst, in_=xo.ap).then_inc(s, 16), 1,
                  "pre_xt%d" % (j % 2), reads=[xo])
        RB = min(2048, S)
        posi = sb("pre_posi", [RB], I32)
        ang = sb("pre_ang", [RB], F32)
        t1 = sb("pre_t1", [RB], F32)
        ni = sb("pre_ni", [RB], I32)
        tb = sb("pre_tb", [RB], F32)
        TWO_PI = 2.0 * math.pi
        for j in range(S // RB):
            src = pos_in[0:1, j * RB:(j + 1) * RB].partition_broadcast(128)
            P.dma("sp", lambda e, s, src=src: e.dma_start(out=posi.ap.unsqueeze(1), in_=src).then_inc(s, 16), 1,
                  "pre_pos", writes=[posi])
            P.op("dve", lambda e: e.tensor_copy(out=ang.ap, in_=posi.ap), reads=[posi], writes=[ang])
            P.op("dve", lambda e: e.tensor_scalar(out=ang.ap, in0=ang.ap, scalar1=ropec.ap[:, 0:1], scalar2=None,
                                                  op0=ALU.mult), reads=[ang, ropec], writes=[ang])
            for which, shift, dstT in (("s", 0.0, ropeS), ("c", 0.5 * math.pi, ropeC)):
                P.op("dve", lambda e, shift=shift: e.tensor_scalar(out=t1.ap, in0=ang.ap, scalar1=shift, scalar2=1.0 / TWO_PI,
                                                                   op0=ALU.add, op1=ALU.mult), reads=[ang], writes=[t1])
                P.op("dve", lambda e: e.tensor_copy(out=ni.ap, in_=t1.ap), reads=[t1], writes=[ni])
                P.op("dve", lambda e: e.tensor_copy(out=t1.ap, in_=ni.ap), reads=[ni], writes=[t1])
                P.op("dve", lambda e: e.tensor_scalar(out=t1.ap, in0=t1.ap, scalar1=-TWO_PI, scalar2=None, op0=ALU.mult),
                     reads=[t1], writes=[t1])
                P.op("dve", lambda e, shift=shift: e.scalar_tensor_tensor(out=tb.ap, in0=ang.ap, scalar=shift, in1=t1.ap,
                                                                          op0=ALU.add, op1=ALU.add),
                     reads=[ang, t1], writes=[tb])
                P.op("dve", lambda e: e.tensor_scalar(out=t1.ap, in0=tb.ap, scalar1=math.pi, scalar2=-TWO_PI,
                                                      op0=ALU.is_gt, op1=ALU.mult), reads=[tb], writes=[t1])
                P.op("dve", lambda e: e.tensor_tensor(out=tb.ap, in0=tb.ap, in1=t1.ap, op=ALU.add),
                     reads=[tb, t1], writes=[tb])
                P.op("dve", lambda e: e.tensor_scalar(out=t1.ap, in0=tb.ap, scalar1=-math.pi, scalar2=TWO_PI,
                                                      op0=ALU.is_lt, op1=ALU.mult), reads=[tb], writes=[t1])
                P.op("dve", lambda e: e.tensor_tensor(out=tb.ap, in0=tb.ap, in1=t1.ap, op=ALU.add),
                     reads=[tb, t1], writes=[tb])
                P.op("dve", lambda e: e.tensor_scalar(out=tb.ap, in0=tb.ap, scalar1=-math.pi, scalar2=math.pi,
                                                      op0=ALU.max, op1=ALU.min), reads=[tb], writes=[tb])
                if which == "s":
                    P.op("act", lambda e: e.activation(out=tb.ap, in_=tb.ap, func=AF.Sin, scale=ropec.ap[:, 1:2]),
                         reads=[tb, ropec], writes=[tb])
                else:
                    P.op("act", lambda e: e.activation(out=tb.ap, in_=tb.ap, func=AF.Sin), reads=[tb], writes=[tb])
                dst = dstT[:, j * RB:(j + 1) * RB]
                P.dma("sp", lambda e, s, dst=dst: e.dma_start(out=dst, in_=tb.ap).then_inc(s, 16), 1,
                      "pre_tb", reads=[tb])

    def emit_norm_block(j, xblk, sq, hT, gcol, rstd, tmp, ps_ssq):
        P.op("pool", lambda e: e.tensor_tensor(out=sq.ap, in0=xblk.ap, in1=xblk.ap, op=ALU.mult),
             reads=[xblk], writes=[sq])

        def f(e):
            for c in range(NCH):
                ins = e.matmul(ps_ssq.ap, lhsT=ones_bf.ap, rhs=sq.ap[:, c, :], start=(c == 0), stop=(c == NCH - 1))
            return ins
        P.op("pe", f, reads=[sq, ones_bf], writes=[ps_ssq])
        P.op("act", lambda e: e.activation(out=tmp.ap, in_=ps_ssq.ap, func=AF.Ln, scale=1.0 / D, bias=epsc.ap),
             reads=[ps_ssq, epsc], writes=[tmp])
        P.op("act", lambda e: e.activation(out=rstd.ap, in_=tmp.ap, func=AF.Exp, scale=-0.5),
             reads=[tmp], writes=[rstd])
        for c in range(NCH):
            P.op("dve", lambda e, c=c: e.scalar_tensor_tensor(out=hT.ap[:, c, :], in0=xblk.ap[:, c, :],
                                                              scalar=gcol.ap[:, c:c + 1], in1=rstd.ap,
                                                              op0=ALU.mult, op1=ALU.mult),
                 reads=[xblk, gcol, rstd], writes=[hT])

    epsc = sb("epsc", [1], F32)
    P.op("dve", lambda e: e.memset(epsc.ap, EPS), writes=[epsc])
    persist_mark = arena.mark()

    def phase_A(l):
        w = sb("A_w", [NCH, INW], BF16)
        wsw = sb("w1", [NCH, 1024], BF16)
        gcol = sb("A_g", [NCH], F32)
        bfc = sb("A_bf", [1], F32)
        xblk = [sb("A_x%d" % i, [NCH, 512], F32) for i in range(2)]
        sq = sb("A_sq", [NCH, 512], BF16)
        hT = [sb("A_h%d" % i, [NCH, 512], BF16) for i in range(2)]
        rstd = sb("A_rstd", [512], F32)
        tmp = sb("A_tmp", [512], F32)
        cS = [sb("A_cS%d" % i, [512], F32) for i in range(2)]
        cC = [sb("A_cC%d" % i, [512], F32) for i in range(2)]
        stg = [sb("A_stg%d" % i, [512], BF16) for i in range(4)]
        r1 = [sb("A_r1_%d" % i, [512], F32) for i in range(2)]
        r2 = [sb("A_r2_%d" % i, [512], F32) for i in range(2)]
        fl1 = sb("A_fl1", [512], F32)
        fl2 = sb("A_fl2", [512], F32)

        wsrc = w_in[l].rearrange("(c p) n -> p c n", p=128)
        def fw(e, s):
            for c in range(NCH):
                for a, b in ((0, 2048), (2048, 4096), (4096, INW)):
                    e.dma_start(out=w.ap[:, c, a:b], in_=wsrc[:, c, a:b]).then_inc(s, 16)
        P.dma("pool", fw, 3 * NCH, "w0", writes=[w])
        P.op("pool", lambda e: e.memset(wsw.ap, 0.0), writes=[wsw])
        qb0 = 1536
        srcg = w.ap[:, :, qb0:qb0 + 1024].rearrange("p c (g k) -> p c g k", k=64)
        dstg = wsw.ap.rearrange("p c (g k) -> p c g k", k=64)
        P.op("pool", lambda e: e.tensor_copy(out=dstg[:, :, :, 8:16], in_=srcg[:, :, :, 0:8]), reads=[w, wsw], writes=[wsw])
        P.op("pool", lambda e: e.tensor_copy(out=dstg[:, :, :, 0:8], in_=srcg[:, :, :, 8:16]), reads=[w, wsw], writes=[wsw])
        def fg(e, s):
            with nc.allow_non_contiguous_dma(reason="small"):
                e.dma_start(out=gcol.ap, in_=norm_mix[l].rearrange("(c p) -> p c", p=128)).then_inc(s, 16)
        P.dma("sp", fg, 1, "g", writes=[gcol])
        P.op("dve", lambda e: e.memset(bfc.ap, 0.0), writes=[bfc])
        def fb(e, s):
            with nc.allow_non_contiguous_dma(reason="small"):
                e.dma_start(out=bfc.ap[120:128, :], in_=b_forget[l].unsqueeze(1)).then_inc(s, 16)
        P.dma("sp", fb, 1, "bf", writes=[bfc])
        P.op("dve", lambda e: e.tensor_scalar(out=bfc.ap, in0=bfc.ap, scalar1=-1.0, scalar2=None, op0=ALU.mult),
             reads=[bfc], writes=[bfc])

        bank = [1]

        def next_bank():
            b = bank[0]
            bank[0] = b + 1 if b < 7 else 1
            return ps[b]
        evac = [0]
        stgi = [0]

        def smalldma(eng, out_ap, in_ap, key, **kw):
            def f(e, s):
                with nc.allow_non_contiguous_dma(reason="small"):
                    e.dma_start(out=out_ap, in_=in_ap).then_inc(s, 16)
            P.dma(eng, f, 1, key, **kw)

        def load_block(j):
            xb_ = xblk[j % 2]
            src = xT[:, :, j * 512:(j + 1) * 512].rearrange("c p t -> p c t")
            P.dma("sp", lambda e, s: e.dma_start(out=xb_.ap, in_=src).then_inc(s, 16), 1, "x%d" % (j % 2), writes=[xb_])
            cs_, cc_ = cS[j % 2], cC[j % 2]
            P.dma("sp", lambda e, s: e.dma_start(out=cs_.ap, in_=ropeS[:, j * 512:(j + 1) * 512]).then_inc(s, 16),
                  1, "cS%d" % (j % 2), writes=[cs_])
            P.dma("sp", lambda e, s: e.dma_start(out=cc_.ap, in_=ropeC[:, j * 512:(j + 1) * 512]).then_inc(s, 16),
                  1, "cC%d" % (j % 2), writes=[cc_])

        def norm_block(j):
            emit_norm_block(j, xblk[j % 2], sq, hT[j % 2], gcol, rstd, tmp, ps[0])

        def store(st, dst):
            k = stgi[0] % 4
            import os
            hk = os.environ.get("HACK", "")
            if hk == "nostore":
                return
            if hk == "noV" and dst.shape[1] == 512 and dst.shape[0] == 128 and getattr(store, "isv", False):
                return
            if hk == "onlyV" and not getattr(store, "isv", False):
                return
            if hk == "poolstore":
                P.dma("pool", lambda e, s: e.dma_start(out=dst, in_=st.ap).then_inc(s, 16), 1, "stg%d" % k, reads=[st])
                return
            if hk == "contig":
                if "dummy" not in _CACHE:
                    _CACHE["dummy"] = nc.dram_tensor("dummyx", [4, 128, 512], BF16).ap()
                dd = _CACHE["dummy"][k]
                P.dma("sp", lambda e, s: e.dma_start(out=dd, in_=st.ap).then_inc(s, 16), 1, "stg%d" % k, reads=[st])
                return
            if hk == "half":
                P.dma("sp", lambda e, s: e.dma_start(out=dst[0:64, :], in_=st.ap[0:64, :]).then_inc(s, 16), 1, "stg%d" % k, reads=[st])
                return
            if hk == "tiny":
                P.dma("sp", lambda e, s: e.dma_start(out=dst[0:1, :], in_=st.ap[0:1, :]).then_inc(s, 16), 1, "stg%d" % k, reads=[st])
                return
            P.dma("sp", lambda e, s: e.dma_start(out=dst, in_=st.ap).then_inc(s, 16), 1, "stg%d" % k, reads=[st])

        def mm_group(pt, lhs_fn, rhs_fn, reads):
            def f(e):
                for c in range(NCH):
                    ins = e.matmul(pt.ap, lhsT=lhs_fn(c), rhs=rhs_fn(c), start=(c == 0), stop=(c == NCH - 1))
                return ins
            P.op("pe", f, reads=reads, writes=[pt])

        def plain_evac(st, pt):
            if evac[0] % 2 == 0:
                P.op("act", lambda e: e.activation(out=st.ap, in_=pt.ap, func=AF.Copy), reads=[pt], writes=[st])
            else:
                P.op("dve", lambda e: e.tensor_copy(out=st.ap, in_=pt.ap), reads=[pt], writes=[st])
            evac[0] += 1

        def proj_fm(j, nm, dstT, col0):
            h_ = hT[j % 2]
            cs_, cc_ = cS[j % 2], cC[j % 2]
            is_b = nm.endswith("b")

            def one(m):
                pt = next_bank()
                cols = slice(col0 + m * 128, col0 + (m + 1) * 128)
                mm_group(pt, lambda c: w.ap[:, c, cols], lambda c: h_.ap[:, c, :], [w, h_])
                st = stg[stgi[0] % 4]
                if not is_b:
                    plain_evac(st, pt)
                else:
                    pt2 = next_bank()
                    sc = slice((col0 - qb0) + m * 128, (col0 - qb0) + (m + 1) * 128)
                    mm_group(pt2, lambda c: wsw.ap[:, c, sc], lambda c: h_.ap[:, c, :], [wsw, h_])
                    a1, a2 = r1[evac[0] % 2], r2[evac[0] % 2]
                    evac[0] += 1
                    P.op("dve", lambda e: e.tensor_tensor(out=a1.ap, in0=pt.ap, in1=cc_.ap, op=ALU.mult),
                         reads=[pt, cc_], writes=[a1])
                    P.op("dve", lambda e: e.tensor_tensor(out=a2.ap, in0=pt2.ap, in1=cs_.ap, op=ALU.mult),
                         reads=[pt2, cs_], writes=[a2])
                    P.op("pool", lambda e: e.tensor_tensor(out=st.ap, in0=a1.ap, in1=a2.ap, op=ALU.add),
                         reads=[a1, a2], writes=[st])
                store(st, dstT[m * 128:(m + 1) * 128, j * 512:(j + 1) * 512])
                stgi[0] += 1
            for m in range(4):
                one(m)

        def proj_v(j, dstV, col0):
            h_ = hT[j % 2]

            def one(tt):
                pt = next_bank()
                mm_group(pt, lambda c: h_.ap[:, c, tt * 128:(tt + 1) * 128], lambda c: w.ap[:, c, col0:col0 + 512], [w, h_])
                st = stg[stgi[0] % 4]
                plain_evac(st, pt)
                store.isv = True
                store(st, dstV[j * 512 + tt * 128: j * 512 + (tt + 1) * 128, :])
                store.isv = False
                stgi[0] += 1
            for tt in range(4):
                one(tt)

        def proj_fl(j):
            h_ = hT[j % 2]
            pt = next_bank()
            mm_group(pt, lambda c: w.ap[:, c, INW - 128:INW], lambda c: h_.ap[:, c, :], [w, h_])
            P.op("act", lambda e: e.activation(out=fl1.ap, in_=pt.ap, func=AF.Exp, scale=-1.0, bias=bfc.ap),
                 reads=[pt, bfc], writes=[fl1])
            P.op("act", lambda e: e.activation(out=fl2.ap, in_=fl1.ap, func=AF.Ln, scale=1.0, bias=ones_f.ap[:, 0:1]),
                 reads=[fl1, ones_f], writes=[fl2])
            P.op("dve", lambda e: e.tensor_scalar(out=fl1.ap, in0=fl2.ap, scalar1=-1.0, scalar2=None, op0=ALU.mult),
                 reads=[fl2], writes=[fl1])
            P.dma("sp", lambda e, s: e.dma_start(out=ls_scr[:, j * 512:(j + 1) * 512], in_=fl1.ap[120:128, :]).then_inc(s, 16),
                  1, "fl", reads=[fl1])

        import os
        NBL = int(os.environ.get("NBLIM", NB))
        load_block(0)
        norm_block(0)
        for j in range(NBL):
            if j + 1 < NB:
                load_block(j + 1)
            proj_fm(j, "QTa", QT["a"], 0)
            proj_fm(j, "KTa", KT["a"], 512)
            proj_fm(j, "QTb", QT["b"], 1536)
            proj_fm(j, "KTb", KT["b"], 2048)
            if j + 1 < NB:
                norm_block(j + 1)
            proj_fm(j, "QTc", QT["c"], 3072)
            proj_fm(j, "KTc", KT["c"], 3584)
            proj_v(j, V["a"], 1024)
            proj_v(j, V["b"], 2560)
            proj_v(j, V["c"], 4096)
            proj_fl(j)

    def phase_C(l):
        PP = 8 * NB
        Lt = sb("B0_L", [512], F32)
        Fl = sb("B0_F", [512], F32)
        one5 = sb("B0_one", [512], F32)
        G8 = sb("B0_G8", [512], BF16)
        offs = sb("B0_off", [1], F32)
        dR = sb("B0_dR", [128], F32)
        BT = sb("B0_BT", [128], F32)
        Fcol2 = sb("C_Fcol", [8 * NT], F32)
        Rbc = sb("C_Rbc", [PP], F32)
        sel = sb("C_sel", [64], F32)
        P.dma("sp", lambda e, s: e.dma_start(out=BT.ap, in_=c_BT).then_inc(s, 16), 1, "g", writes=[BT])
        P.op("dve", lambda e: e.memset(one5.ap, 1.0), writes=[one5])
        P.op("dve", lambda e: e.memset(sel.ap, 0.0), writes=[sel])
        P.op("dve", lambda e: e.memset(sel.ap[64:65, :], 1.0), reads=[sel], writes=[sel])
        P.dma("sp", lambda e, s: e.dma_start(out=Lt.ap[0:PP, :], in_=ls_scr.rearrange("h (s t) -> (h s) t", t=512)).then_inc(s, 16),
              1, "x0", writes=[Lt])
        P.op("dve", lambda e: e.tensor_tensor_scan(out=Fl.ap[0:PP, :], data0=one5.ap[0:PP, :], data1=Lt.ap[0:PP, :], initial=0.0,
                                                   op0=ALU.mult, op1=ALU.add), reads=[one5, Lt], writes=[Fl])
        P.op("pe", lambda e: e.matmul(ps[0].ap[0:PP, 0:1], lhsT=BT.ap[0:PP, 0:PP], rhs=Fl.ap[0:PP, 511:512], start=True, stop=True),
             reads=[BT, Fl], writes=[ps[0]])
        P.op("dve", lambda e: e.tensor_copy(out=offs.ap[0:PP, :], in_=ps[0].ap[0:PP, 0:1]), reads=[ps[0]], writes=[offs])
        P.op("dve", lambda e: e.tensor_scalar(out=Fl.ap[0:PP, :], in0=Fl.ap[0:PP, :], scalar1=offs.ap[0:PP, 0:1], scalar2=None,
                                              op0=ALU.add), reads=[Fl, offs], writes=[Fl])
        P.op("dve", lambda e: e.tensor_scalar(out=G8.ap[0:PP, :], in0=Fl.ap[0:PP, :], scalar1=Fl.ap[0:PP, 255:256], scalar2=8.0,
                                              op0=ALU.subtract, op1=ALU.mult), reads=[Fl], writes=[G8])
        P.dma("sp", lambda e, s: e.dma_start(out=Gq.rearrange("h (s t) -> (h s) t", t=512), in_=G8.ap[0:PP, :]).then_inc(s, 16),
              1, "x1", reads=[G8])
        gq_store = P.dma_keys["x1"][1]

        def ftr(e):
            for jj in range(4):
                ins = e.transpose(out=ps[1].ap[:, jj * PP:(jj + 1) * PP], in_=Fl.ap[0:PP, jj * 128:(jj + 1) * 128],
                                  identity=ident.ap[0:PP, 0:PP])
            return ins
        P.op("pe", ftr, reads=[Fl, ident], writes=[ps[1]])
        P.op("dve", lambda e: e.tensor_copy(out=Fcol2.ap.rearrange("p (h s j) -> p j h s", h=8, s=NB, j=4),
                                            in_=ps[1].ap[:, 0:4 * PP].rearrange("p (j h s) -> p j h s", j=4, h=8, s=NB)),
             reads=[ps[1]], writes=[Fcol2])
        P.op("dve", lambda e: e.tensor_scalar(out=dR.ap[0:PP, 0:PP], in0=ident.ap[0:PP, 0:PP], scalar1=Fl.ap[0:PP, 255:256],
                                              scalar2=None, op0=ALU.mult), reads=[ident, Fl], writes=[dR])
        P.op("pe", lambda e: e.matmul(ps[2].ap[:, 0:PP], lhsT=ones_f.ap[0:PP, :], rhs=dR.ap[0:PP, 0:PP], start=True, stop=True),
             reads=[ones_f, dR], writes=[ps[2]])
        P.op("dve", lambda e: e.tensor_copy(out=Rbc.ap, in_=ps[2].ap[:, 0:PP]), reads=[ps[2]], writes=[Rbc])

        Vaug = sb("C_V", [NT, 8, 65], BF16)
        VC = min(8, NT)
        vst = sb("C_vst", [VC, 512], BF16)
        P.op("pool", lambda e: e.memset(Vaug.ap[:, :, :, 64:65], 1.0), writes=[Vaug])
        for t0 in range(0, NT, VC):
            def fv(t0):
                src = V["c"][t0 * 128:(t0 + VC) * 128, :].rearrange("(t p) c -> p t c", p=128)
                P.dma("sp", lambda e, s: e.dma_start(out=vst.ap, in_=src).then_inc(s, 16), 1, "x0", writes=[vst])
                P.op("pool", lambda e: e.tensor_copy(out=Vaug.ap[:, t0:t0 + VC, :, 0:64],
                                                     in_=vst.ap.rearrange("p t (h d) -> p t h d", h=8)),
                     reads=[vst, Vaug], writes=[Vaug])
            fv(t0)
        mk = sb("C_mask", [4, 512], F32)
        P.dma("sp", lambda e, s: e.dma_start(out=mk.ap, in_=c_maskC.rearrange("i k q -> k i q")).then_inc(s, 16), 1, "g", writes=[mk])

        QTh = [sb("C_Q%d" % i, [S], BF16) for i in range(2)]
        KTh = [sb("C_K%d" % i, [S], BF16) for i in range(2)]
        for i in range(2):
            P.op("pool", lambda e, i=i: e.memset(KTh[i].ap[64:128, :], 0.0), writes=[KTh[i]])
            P.op("pool", lambda e, i=i: e.memset(KTh[i].ap[64:65, :], 1.0), reads=[KTh[i]], writes=[KTh[i]])
            P.op("pool", lambda e, i=i: e.memset(QTh[i].ap[64:128, :], 0.0), writes=[QTh[i]])
        pT = [sb("C_p%d" % i, [512], BF16) for i in range(4)]
        dtmp = [sb("C_dt%d" % i, [512], F32) for i in range(2)]
        bcol = [sb("C_bc%d" % i, [NT], F32) for i in range(2)]
        osb = [sb("C_o%d" % i, [512], F32) for i in range(2)]
        rzb = [sb("C_rz%d" % i, [512], F32) for i in range(2)]
        yst = [sb("C_y%d" % i, [512], BF16) for i in range(2)]

        def load_head(h):
            q_, k_ = QTh[h % 2], KTh[h % 2]

            def fq(e, s):
                e.dma_start(out=q_.ap[0:64, :], in_=QT["c"][h * 64:(h + 1) * 64, :]).then_inc(s, 16)
                e.dma_start(out=q_.ap[64:65, :], in_=Gq[h:h + 1, :]).then_inc(s, 16)
            o = P.dma("sp", fq, 2, "q%d" % (h % 2), reads=[q_], writes=[q_])
            if gq_store not in o.deps:
                o.deps.append(gq_store)
            P.dma("sp", lambda e, s: e.dma_start(out=k_.ap[0:64, :], in_=KT["c"][h * 64:(h + 1) * 64, :]).then_inc(s, 16),
                  1, "k%d" % (h % 2), reads=[k_], writes=[k_])

        pairs = []
        for h in range(8):
            for qb in range(NB):
                nk = 4 * qb + 4
                for kt in range(nk):
                    pairs.append((h, qb, kt, kt == 0, kt == nk - 1))
        LA = 2
        pending = []
        gidx = [0]

        def emit_front(n):
            h, qb, kt, first, last = pairs[n]
            if first and qb == 0:
                if h == 0:
                    load_head(0)
                if h + 1 < 8:
                    load_head(h + 1)
            g = h * NB + qb
            if first:
                bc = bcol[g % 2]
                P.op("dve", lambda e: e.tensor_scalar(out=bc.ap[:, 0:4 * qb + 4], in0=Fcol2.ap[:, h * NT:h * NT + 4 * qb + 4],
                                                      scalar1=-1.0, scalar2=Rbc.ap[:, g:g + 1], op0=ALU.mult, op1=ALU.add),
                     reads=[Fcol2, Rbc], writes=[bc])
            bc = bcol[g % 2]
            q_, k_ = QTh[h % 2], KTh[h % 2]
            pt = ps[n % 4]
            di = kt - 4 * qb
            c0 = 128 * di if di > 0 else 0
            qs = slice(c0, 512)
            P.op("pe", lambda e: e.matmul(pt.ap[:, qs], lhsT=k_.ap[:, kt * 128:(kt + 1) * 128],
                                          rhs=q_.ap[:, qb * 512 + c0:(qb + 1) * 512], start=True, stop=True),
                 reads=[k_, q_], writes=[pt])
            p_ = pT[n % 4]
            if di >= 0:
                dt_ = dtmp[n % 2]
                P.op("dve", lambda e: e.tensor_tensor(out=dt_.ap[:, qs], in0=pt.ap[:, qs], in1=mk.ap[:, di, qs], op=ALU.add),
                     reads=[pt, mk], writes=[dt_])
                P.op("act", lambda e: e.activation(out=p_.ap[:, qs], in_=dt_.ap[:, qs], func=AF.Exp, scale=0.125, bias=bc.ap[:, kt:kt + 1]),
                     reads=[dt_, bc], writes=[p_])
            else:
                P.op("act", lambda e: e.activation(out=p_.ap, in_=pt.ap, func=AF.Exp, scale=0.125, bias=bc.ap[:, kt:kt + 1]),
                     reads=[pt, bc], writes=[p_])

        def emit_back(n):
            h, qb, kt, first, last = pairs[n]
            g = h * NB + qb
            po = ps[4 + g % 2]
            p_ = pT[n % 4]
            di = kt - 4 * qb
            qs = slice(128 * di if di > 0 else 0, 512)
            P.op("pe", lambda e: e.matmul(po.ap[0:65, qs], lhsT=Vaug.ap[:, kt, h, :], rhs=p_.ap[:, qs], start=first, stop=last),
                 reads=[Vaug, p_], writes=[po])
            if last:
                o_, rz_, y_ = osb[g % 2], rzb[g % 2], yst[g % 2]
                P.op("dve", lambda e: e.tensor_copy(out=o_.ap[0:65, :], in_=po.ap[0:65, :]), reads=[po], writes=[o_])

                def later():
                    P.op("pe", lambda e: e.matmul(ps[6].ap[0:64, :], lhsT=sel.ap[0:65, :], rhs=o_.ap[0:65, :], start=True, stop=True),
                         reads=[sel, o_], writes=[ps[6]])
                    P.op("dve", lambda e: e.reciprocal(out=rz_.ap[0:64, :], in_=ps[6].ap[0:64, :]), reads=[ps[6]], writes=[rz_])
                    P.op("dve", lambda e: e.tensor_tensor(out=y_.ap[0:64, :], in0=o_.ap[0:64, :], in1=rz_.ap[0:64, :], op=ALU.mult),
                         reads=[o_, rz_], writes=[y_])
                    dst = yT[1024 + h * 64:1024 + (h + 1) * 64, qb * 512:(qb + 1) * 512]
                    P.dma("sp", lambda e, s: e.dma_start(out=dst, in_=y_.ap[0:64, :]).then_inc(s, 16), 1, "y%d" % (g % 2), reads=[y_])
                pending.append([2, later])

        for n in range(len(pairs) + LA):
            if n < len(pairs):
                emit_front(n)
            if n - LA >= 0:
                emit_back(n - LA)
            for it in list(pending):
                it[0] -= 1
                if it[0] <= 0:
                    it[1]()
                    pending.remove(it)
        for it in pending:
            it[1]()

    def phase_B(l):
        lam_init = 0.8 - 0.6 * math.exp(-0.3 * l)
        lamt = sb("B_lam", [256], F32)
        lprod = sb("B_lprod", [128], F32)
        lsum = sb("B_lsum", [2], F32)
        neglam = sb("B_neglam", [1], F32)
        sgc = sb("B_sgc", [1], F32)
        epsb = sb("B_eps", [1], F32)

        def fl_(e, s):
            with nc.allow_non_contiguous_dma(reason="small"):
                e.dma_start(out=lamt.ap.unsqueeze(1), in_=lambdas[l].partition_broadcast(128)).then_inc(s, 16)
                e.dma_start(out=sgc.ap, in_=sub_gain[l].unsqueeze(1)).then_inc(s, 16)
        P.dma("sp", fl_, 2, "g", writes=[lamt, sgc])
        P.op("dve", lambda e: e.tensor_tensor(out=lprod.ap[:, 0:64], in0=lamt.ap[:, 0:64], in1=lamt.ap[:, 64:128], op=ALU.mult),
             reads=[lamt], writes=[lprod])
        P.op("dve", lambda e: e.tensor_tensor(out=lprod.ap[:, 64:128], in0=lamt.ap[:, 128:192], in1=lamt.ap[:, 192:256], op=ALU.mult),
             reads=[lamt, lprod], writes=[lprod])
        P.op("dve", lambda e: e.reduce_sum(out=lsum.ap, in_=lprod.ap.rearrange("p (a b) -> p a b", a=2), axis=mybir.AxisListType.X),
             reads=[lprod], writes=[lsum])
        P.op("act", lambda e: e.activation(out=lsum.ap, in_=lsum.ap, func=AF.Exp), reads=[lsum], writes=[lsum])
        P.op("dve", lambda e: e.tensor_tensor(out=neglam.ap, in0=lsum.ap[:, 1:2], in1=lsum.ap[:, 0:1], op=ALU.subtract),
             reads=[lsum], writes=[neglam])
        P.op("dve", lambda e: e.tensor_scalar(out=neglam.ap, in0=neglam.ap, scalar1=-lam_init, scalar2=None, op0=ALU.add),
             reads=[neglam], writes=[neglam])
        P.op("dve", lambda e: e.tensor_scalar(out=sgc.ap, in0=sgc.ap, scalar1=1.0 - lam_init, scalar2=None, op0=ALU.mult),
             reads=[sgc], writes=[sgc])
        P.op("dve", lambda e: e.memset(epsb.ap, EPS), writes=[epsb])

        mk = sb("B_mask", [4, 512], F32)
        P.dma("sp", lambda e, s: e.dma_start(out=mk.ap, in_=c_maskB.rearrange("i k q -> k i q")).then_inc(s, 16), 1, "g", writes=[mk])
        Qh = [sb("B_Q%d" % i, [S], BF16) for i in range(2)]
        K0 = [sb("B_K0%d" % i, [S], BF16) for i in range(2)]
        K1 = [sb("B_K1%d" % i, [S], BF16) for i in range(2)]
        Vh = [sb("B_V%d" % i, [NT, 128], BF16) for i in range(2)]
        for i in range(2):
            P.op("pool", lambda e, i=i: e.memset(K0[i].ap[64:128, :], 0.0), writes=[K0[i]])
            P.op("pool", lambda e, i=i: e.memset(K1[i].ap[0:64, :], 0.0), writes=[K1[i]])
        pT = [sb("B_p%d" % i, [2, 512], BF16) for i in range(3)]
        zacc = [sb("B_z%d" % i, [2, 512], F32) for i in range(2)]
        zaccp = [P.tile("B_zp%d" % i, None) for i in range(2)]
        dtmp = [sb("B_dt%d" % i, [2, 512], F32) for i in range(2)]
        o0b = [sb("B_o0%d" % i, [512], F32) for i in range(2)]
        o1b = [sb("B_o1%d" % i, [512], F32) for i in range(2)]
        r0b = [sb("B_r0%d" % i, [512], F32) for i in range(2)]
        r1b = [sb("B_r1%d" % i, [512], F32) for i in range(2)]
        sqb = [sb("B_sq%d" % i, [512], F32) for i in range(2)]
        yst = [sb("B_y%d" % i, [512], BF16) for i in range(2)]

        def load_head(h):
            q_, k0_, k1_, v_ = Qh[h % 2], K0[h % 2], K1[h % 2], Vh[h % 2]
            P.dma("sp", lambda e, s: e.dma_start(out=q_.ap, in_=QT["b"][h * 128:(h + 1) * 128, :]).then_inc(s, 16),
                  1, "q%d" % (h % 2), writes=[q_])
            P.dma("sp", lambda e, s: e.dma_start(out=k0_.ap[0:64, :], in_=KT["b"][h * 128:h * 128 + 64, :]).then_inc(s, 16),
                  1, "k%d" % (h % 2), reads=[k0_], writes=[k0_])
            P.dma("sp", lambda e, s: e.dma_start(out=k1_.ap[64:128, :], in_=KT["b"][h * 128 + 64:(h + 1) * 128, :]).then_inc(s, 16),
                  1, "kk%d" % (h % 2), reads=[k1_], writes=[k1_])
            srcv = V["b"][:, h * 128:(h + 1) * 128].rearrange("(t p) c -> p t c", p=128)
            TC = min(8, NT)

            def fvl(e, s):
                for t0 in range(0, NT, TC):
                    e.dma_start(out=v_.ap[:, t0:t0 + TC, :], in_=srcv[:, t0:t0 + TC, :]).then_inc(s, 16)
            P.dma("sp", fvl, NT // TC, "cS%d" % (h % 2), writes=[v_])

        pairs = []
        for h in range(4):
            for qb in range(NB):
                nk = 4 * qb + 4
                for kt in range(nk):
                    pairs.append((h, qb, kt, kt == 0, kt == nk - 1))
        LA = 1
        pending = []
        sidx = [0]
        smap = {}

        def emit_front(n):
            h, qb, kt, first, last = pairs[n]
            q_ = Qh[h % 2]
            di = kt - 4 * qb
            c0 = 128 * di if di > 0 else 0
            qs = slice(c0, 512)
            slot = n % 2
            b0, b1 = ps[2 * slot], ps[2 * slot + 1]
            p_ = pT[n % 3]
            smap[n] = p_
            for c, k_, pt in ((0, K0[h % 2], b0), (1, K1[h % 2], b1)):
                def one(c, k_, pt):
                    P.op("pe", lambda e: e.matmul(pt.ap[:, qs], lhsT=k_.ap[:, kt * 128:(kt + 1) * 128],
                                                  rhs=q_.ap[:, qb * 512 + c0:(qb + 1) * 512],
                                                  start=True, stop=True), reads=[k_, q_], writes=[pt])
                one(c, k_, pt)
            both = psall[:, 2 * slot * 512:(2 * slot + 2) * 512].rearrange("p (c q) -> p c q", c=2)
            if di >= 0:
                dt_ = dtmp[n % 2]
                for c, pt in ((0, b0), (1, b1)):
                    def two(c, pt):
                        P.op("dve", lambda e: e.tensor_tensor(out=dt_.ap[:, c, qs], in0=pt.ap[:, qs], in1=mk.ap[:, di, qs], op=ALU.add),
                             reads=[pt, mk], writes=[dt_])
                    two(c, pt)
                P.op("act", lambda e: e.activation(out=p_.ap[:, :, qs], in_=dt_.ap[:, :, qs], func=AF.Exp, scale=0.125),
                     reads=[dt_], writes=[p_])
            else:
                P.op("act", lambda e: e.activation(out=p_.ap, in_=both, func=AF.Exp, scale=0.125),
                     reads=[b0, b1], writes=[p_])

        def emit_back(n):
            h, qb, kt, first, last = pairs[n]
            g = h * NB + qb
            v_ = Vh[h % 2]
            di = kt - 4 * qb
            qs = slice(128 * di if di > 0 else 0, 512)
            p_ = smap.pop(n)
            za = zacc[g % 2]
            for c in (0, 1):
                def one(c):
                    po = ps[4 + c]
                    P.op("pe", lambda e: e.matmul(po.ap[:, qs], lhsT=v_.ap[:, kt, :], rhs=p_.ap[:, c, qs], start=first, stop=last),
                         reads=[v_, p_], writes=[po])
                one(c)
            zb = zaccp[g % 2]
            SPL = 344
            if first:
                P.op("dve", lambda e: e.tensor_copy(out=za.ap[:, :, 0:SPL], in_=p_.ap[:, :, 0:SPL]), reads=[p_], writes=[za])
                P.op("pool", lambda e: e.tensor_copy(out=za.ap[:, :, SPL:512], in_=p_.ap[:, :, SPL:512]), reads=[p_], writes=[zb])
            elif di > 0:
                P.op("dve", lambda e: e.tensor_tensor(out=za.ap[:, :, qs], in0=za.ap[:, :, qs], in1=p_.ap[:, :, qs], op=ALU.add),
                     reads=[za, zb, p_], writes=[za, zb])
            else:
                P.op("dve", lambda e: e.tensor_tensor(out=za.ap[:, :, 0:SPL], in0=za.ap[:, :, 0:SPL], in1=p_.ap[:, :, 0:SPL], op=ALU.add),
                     reads=[za, p_], writes=[za])
                P.op("pool", lambda e: e.tensor_tensor(out=za.ap[:, :, SPL:512], in0=za.ap[:, :, SPL:512], in1=p_.ap[:, :, SPL:512], op=ALU.add),
                     reads=[zb, p_], writes=[zb])
            if last:
                o0, o1, r0, r1, sq_, y_ = o0b[g % 2], o1b[g % 2], r0b[g % 2], r1b[g % 2], sqb[g % 2], yst[g % 2]
                zz = zacc[g % 2]
                P.op("act", lambda e: e.activation(out=o0.ap, in_=ps[4].ap, func=AF.Copy), reads=[ps[4]], writes=[o0])
                P.op("act", lambda e: e.activation(out=o1.ap, in_=ps[5].ap, func=AF.Copy), reads=[ps[5]], writes=[o1])

                def later2():
                    P.op("pe", lambda e: e.matmul(ps[7].ap, lhsT=ones_f.ap, rhs=sq_.ap, start=True, stop=True),
                         reads=[ones_f, sq_], writes=[ps[7]])
                    P.op("act", lambda e: e.activation(out=r0.ap, in_=ps[7].ap, func=AF.Ln, scale=1.0 / 128, bias=epsb.ap),
                         reads=[ps[7], epsb], writes=[r0])
                    P.op("act", lambda e: e.activation(out=r1.ap, in_=r0.ap, func=AF.Exp, scale=-0.5), reads=[r0], writes=[r1])
                    P.op("dve", lambda e: e.scalar_tensor_tensor(out=y_.ap, in0=o0.ap, scalar=sgc.ap[:, 0:1], in1=r1.ap,
                                                                 op0=ALU.mult, op1=ALU.mult), reads=[o0, sgc, r1], writes=[y_])
                    dst = yT[512 + h * 128:512 + (h + 1) * 128, qb * 512:(qb + 1) * 512]
                    P.dma("sp", lambda e, s: e.dma_start(out=dst, in_=y_.ap).then_inc(s, 16), 1, "y%d" % (g % 2), reads=[y_])

                def later1():
                    P.op("pe", lambda e: e.matmul(ps[7].ap, lhsT=ones_f.ap, rhs=zz.ap[:, 0, :], start=True, stop=True),
                         reads=[ones_f, zz, zaccp[g % 2]], writes=[ps[7]])
                    P.op("pe", lambda e: e.matmul(ps[6].ap, lhsT=ones_f.ap, rhs=zz.ap[:, 1, :], start=True, stop=True),
                         reads=[ones_f, zz, zaccp[g % 2]], writes=[ps[6]])
                    P.op("act", lambda e: e.activation(out=r0.ap, in_=ps[7].ap, func=AF.Ln), reads=[ps[7]], writes=[r0])
                    P.op("act", lambda e: e.activation(out=r0.ap, in_=r0.ap, func=AF.Exp, scale=-1.0), reads=[r0], writes=[r0])
                    P.op("act", lambda e: e.activation(out=r1.ap, in_=ps[6].ap, func=AF.Ln), reads=[ps[6]], writes=[r1])
                    P.op("act", lambda e: e.activation(out=r1.ap, in_=r1.ap, func=AF.Exp, scale=-1.0), reads=[r1], writes=[r1])
                    P.op("dve", lambda e: e.tensor_tensor(out=o0.ap, in0=o0.ap, in1=r0.ap, op=ALU.mult), reads=[o0, r0], writes=[o0])
                    P.op("dve", lambda e: e.tensor_tensor(out=o1.ap, in0=o1.ap, in1=r1.ap, op=ALU.mult), reads=[o1, r1], writes=[o1])
                    P.op("dve", lambda e: e.scalar_tensor_tensor(out=o0.ap, in0=o1.ap, scalar=neglam.ap[:, 0:1], in1=o0.ap,
                                                                 op0=ALU.mult, op1=ALU.add), reads=[o1, o0, neglam], writes=[o0])
                    P.op("pool", lambda e: e.tensor_tensor(out=sq_.ap, in0=o0.ap, in1=o0.ap, op=ALU.mult), reads=[o0], writes=[sq_])
                    pending.append([3, later2])
                pending.append([2, later1])

        load_head(0)
        for n in range(len(pairs) + LA):
            if n < len(pairs):
                emit_front(n)
            if n - LA >= 0:
                emit_back(n - LA)
                hh, qq, kk, ff, ll = pairs[n - LA]
                if ff and qq == 0 and hh + 1 < 4:
                    load_head(hh + 1)
            for it in list(pending):
                it[0] -= 1
                if it[0] <= 0:
                    it[1]()
                    pending.remove(it)
        while pending:
            it = pending.pop(0)
            it[1]()

    def phase_M(l):
        tabR = sb("M_tab", [5, 8, 128], F32)

        def ft(e, s):
            for o in range(5):
                e.dma_start(out=tabR.ap[:, o, :, :], in_=biasA[l, 4 - o]).then_inc(s, 16)
        P.dma("sp", ft, 5, "g", writes=[tabR])
        tab8 = sb("M_tab8", [5, 8, 128], BF16)
        idb = sb("M_idb", [128], BF16)
        P.op("dve", lambda e: e.tensor_scalar(out=tab8.ap, in0=tabR.ap, scalar1=8.0, scalar2=None, op0=ALU.mult),
             reads=[tabR], writes=[tab8])
        P.op("dve", lambda e: e.tensor_copy(out=idb.ap, in_=ident.ap), reads=[ident], writes=[idb])
        sel = sb("M_sel", [64], F32)
        P.op("dve", lambda e: e.memset(sel.ap, 0.0), writes=[sel])
        P.op("dve", lambda e: e.memset(sel.ap[64:65, :], 1.0), reads=[sel], writes=[sel])
        Qh = [sb("M_Q%d" % i, [S], BF16) for i in range(2)]
        K0 = [sb("M_K0%d" % i, [S], BF16) for i in range(2)]
        K1 = [sb("M_K1%d" % i, [S], BF16) for i in range(2)]
        Vh = [sb("M_V%d" % i, [NT, 2, 65], BF16) for i in range(2)]
        vst1 = sb("M_vst", [NT, 128], BF16)
        vst = [vst1, vst1]
        for i in range(2):
            P.op("pool", lambda e, i=i: e.memset(K0[i].ap[64:128, :], 0.0), writes=[K0[i]])
            P.op("pool", lambda e, i=i: e.memset(K1[i].ap[0:64, :], 0.0), writes=[K1[i]])
            P.op("pool", lambda e, i=i: e.memset(Vh[i].ap[:, :, :, 64:65], 1.0), writes=[Vh[i]])
        pT = [sb("M_p%d" % i, [512], BF16) for i in range(4)]
        osb = [sb("M_o%d" % i, [512], F32) for i in range(2)]
        rzb = [sb("M_rz%d" % i, [512], F32) for i in range(2)]
        yst = [sb("M_y%d" % i, [512], BF16) for i in range(2)]

        def load_pair(hp):
            q_, k0_, k1_, v_, vs_ = Qh[hp % 2], K0[hp % 2], K1[hp % 2], Vh[hp % 2], vst[hp % 2]
            P.dma("sp", lambda e, s: e.dma_start(out=q_.ap, in_=QT["a"][hp * 128:(hp + 1) * 128, :]).then_inc(s, 16),
                  1, "q%d" % (hp % 2), writes=[q_])
            P.dma("sp", lambda e, s: e.dma_start(out=k0_.ap[0:64, :], in_=KT["a"][hp * 128:hp * 128 + 64, :]).then_inc(s, 16),
                  1, "k%d" % (hp % 2), reads=[k0_], writes=[k0_])
            P.dma("sp", lambda e, s: e.dma_start(out=k1_.ap[64:128, :], in_=KT["a"][hp * 128 + 64:(hp + 1) * 128, :]).then_inc(s, 16),
                  1, "kk%d" % (hp % 2), reads=[k1_], writes=[k1_])
            srcv = V["a"][:, hp * 128:(hp + 1) * 128].rearrange("(t p) c -> p t c", p=128)
            TC = min(8, NT)

            def fvl(e, s):
                for t0 in range(0, NT, TC):
                    e.dma_start(out=vs_.ap[:, t0:t0 + TC, :], in_=srcv[:, t0:t0 + TC, :]).then_inc(s, 16)
            P.dma("sp", fvl, NT // TC, "cS%d" % (hp % 2), writes=[vs_])
            P.op("pool", lambda e: e.tensor_copy(out=v_.ap[:, :, :, 0:64], in_=vs_.ap.rearrange("p t (h d) -> p t h d", h=2)),
                 reads=[vs_, v_], writes=[v_])

        pairs = []
        for hp in range(4):
            for hd in range(2):
                for qb in range(NB):
                    order = [3, 0, 1, 2, 4, 5, 6, 7] if qb >= 1 else [4, 5, 6, 7]
                    for ii, i in enumerate(order):
                        pairs.append((hp, hd, qb, i, ii == 0, ii == len(order) - 1))
        LA = 2
        pending = []

        def geom(qb, i):
            a_, b_ = max(0, i - 4), min(3, i)
            return a_, b_, 4 * qb - 4 + i

        def emit_front(n):
            hp, hd, qb, i, first, last = pairs[n]
            a_, b_, kt = geom(qb, i)
            h = 2 * hp + hd
            q_ = Qh[hp % 2]
            k_ = (K0 if hd == 0 else K1)[hp % 2]
            pt = ps[n % 4]
            p_ = pT[n % 4]
            qs = slice(128 * a_, 128 * (b_ + 1))
            def fqk(e):
                e.matmul(pt.ap[:, qs], lhsT=k_.ap[:, kt * 128:(kt + 1) * 128],
                         rhs=q_.ap[:, qb * 512 + 128 * a_:qb * 512 + 128 * (b_ + 1)], start=True, stop=False)
                for t_ in range(a_, b_ + 1):
                    ins = e.matmul(pt.ap[:, 128 * t_:128 * (t_ + 1)], lhsT=idb.ap, rhs=tab8.ap[:, t_ + 4 - i, h, :],
                                   start=False, stop=(t_ == b_))
                return ins
            P.op("pe", fqk, reads=[k_, q_, idb, tab8], writes=[pt])
            P.op("act", lambda e: e.activation(out=p_.ap[:, qs], in_=pt.ap[:, qs], func=AF.Exp, scale=0.125), reads=[pt], writes=[p_])

        def emit_back(n):
            hp, hd, qb, i, first, last = pairs[n]
            a_, b_, kt = geom(qb, i)
            h = 2 * hp + hd
            g = h * NB + qb
            po = ps[4 + g % 2]
            p_ = pT[n % 4]
            v_ = Vh[hp % 2]
            qs = slice(128 * a_, 128 * (b_ + 1))
            P.op("pe", lambda e: e.matmul(po.ap[0:65, qs], lhsT=v_.ap[:, kt, hd, :], rhs=p_.ap[:, qs], start=first, stop=last),
                 reads=[v_, p_], writes=[po])
            if last:
                o_, rz_, y_ = osb[g % 2], rzb[g % 2], yst[g % 2]
                P.op("dve", lambda e: e.tensor_copy(out=o_.ap[0:65, :], in_=po.ap[0:65, :]), reads=[po], writes=[o_])

                def later():
                    P.op("pe", lambda e: e.matmul(ps[6].ap[0:64, :], lhsT=sel.ap[0:65, :], rhs=o_.ap[0:65, :], start=True, stop=True),
                         reads=[sel, o_], writes=[ps[6]])
                    P.op("dve", lambda e: e.reciprocal(out=rz_.ap[0:64, :], in_=ps[6].ap[0:64, :]), reads=[ps[6]], writes=[rz_])
                    P.op("dve", lambda e: e.tensor_tensor(out=y_.ap[0:64, :], in0=o_.ap[0:64, :], in1=rz_.ap[0:64, :], op=ALU.mult),
                         reads=[o_, rz_], writes=[y_])
                    dst = yT[h * 64:(h + 1) * 64, qb * 512:(qb + 1) * 512]
                    P.dma("sp", lambda e, s: e.dma_start(out=dst, in_=y_.ap[0:64, :]).then_inc(s, 16), 1, "y%d" % (g % 2), reads=[y_])
                pending.append([2, later])

        load_pair(0)
        for n in range(len(pairs) + LA):
            if n < len(pairs):
                emit_front(n)
            if n - LA >= 0:
                emit_back(n - LA)
                hp_, hd_, qq, i_, ff, ll = pairs[n - LA]
                if ff and qq == 0 and hd_ == 0 and hp_ + 1 < 4:
                    load_pair(hp_ + 1)
            for it in list(pending):
                it[0] -= 1
                if it[0] <= 0:
                    it[1]()
                    pending.remove(it)
        for it in pending:
            it[1]()

    def load_w_cast(dst_tile, src3, ncols, key):
        nch = src3.shape[1]
        pieces = [(a_, min(a_ + 2048, ncols)) for a_ in range(0, ncols, 2048)]

        def f(e, s):
            for c in range(nch):
                for a_, b_ in pieces:
                    e.dma_start(out=dst_tile.ap[:, c, a_:b_], in_=src3[:, c, a_:b_]).then_inc(s, 16)
        P.dma("pool", f, nch * len(pieces), key, writes=[dst_tile])

    def col_load(dst_tile, src1, ncol, key):
        def f(e, s):
            with nc.allow_non_contiguous_dma(reason="small"):
                e.dma_start(out=dst_tile.ap, in_=src1.rearrange("(c p) -> p c", p=128)).then_inc(s, 16)
        P.dma("sp", f, 1, key, writes=[dst_tile])

    def phase_D1(l):
        wg = sb("D_wg", [NCH, 3 * D], BF16)
        wb = [sb("D_wb%d" % i, [4, D], BF16) for i in range(3)]
        wo = sb("D_wo", [NCH, D], BF16)
        bg = sb("D_bg", [24], F32)
        gcol = sb("D_g", [NCH], F32)
        xblk = [sb("D_x%d" % i, [NCH, 512], F32) for i in range(2)]
        sq = sb("D_sq", [NCH, 512], BF16)
        hT = [sb("D_h%d" % i, [NCH, 512], BF16) for i in range(2)]
        rstd = sb("D_rstd", [512], F32)
        tmp = sb("D_tmp", [512], F32)
        yblk = [sb("D_y%d" % i, [12, 512], BF16) for i in range(2)]
        mg = sb("D_mg", [NCH, 512], BF16)
        gb = [sb("D_gb%d" % i, [512], F32) for i in range(3)]
        macc = [sb("D_m%d" % i, [512], F32) for i in range(2)]
        tt_ = [sb("D_t%d" % i, [512], F32) for i in range(2)]
        load_w_cast(wg, w_gate[l].rearrange("(c p) n -> p c n", p=128), 3 * D, "w0")
        for i in range(3):
            load_w_cast(wb[i], w_br[i][l].rearrange("(c p) n -> p c n", p=128), D, "w1")
        load_w_cast(wo, w_out[l].rearrange("(c p) n -> p c n", p=128), D, "w0")
        col_load(bg, b_gate[l], 24, "g")
        col_load(gcol, norm_mix[l], NCH, "bf")

        bank = [1]

        def next_bank():
            b = bank[0]
            bank[0] = b + 1 if b < 7 else 1
            return ps[b]
        cnt = [0]

        def mm_group(pt, n, lhs_fn, rhs_fn, reads):
            def f(e):
                for c in range(n):
                    ins = e.matmul(pt.ap, lhsT=lhs_fn(c), rhs=rhs_fn(c), start=(c == 0), stop=(c == n - 1))
                return ins
            P.op("pe", f, reads=reads, writes=[pt])

        def load_block(j):
            xb_, yb_ = xblk[j % 2], yblk[j % 2]
            src = xT[:, :, j * 512:(j + 1) * 512].rearrange("c p t -> p c t")
            P.dma("sp", lambda e, s: e.dma_start(out=xb_.ap, in_=src).then_inc(s, 16), 1, "x%d" % (j % 2), writes=[xb_])
            srcy = yT[:, j * 512:(j + 1) * 512].rearrange("(c p) t -> p c t", p=128)
            P.dma("sp", lambda e, s: e.dma_start(out=yb_.ap, in_=srcy).then_inc(s, 16), 1, "cC%d" % (j % 2), writes=[yb_])

        def norm_block(j):
            emit_norm_block(j, xblk[j % 2], sq, hT[j % 2], gcol, rstd, tmp, ps[0])

        def merge_chunk(j, dch):
            h_, yb_ = hT[j % 2], yblk[j % 2]
            m_ = macc[dch % 2]
            for br in range(3):
                def one(br):
                    pg = next_bank()
                    col = br * D + dch * 128
                    mm_group(pg, NCH, lambda c: wg.ap[:, c, col:col + 128], lambda c: h_.ap[:, c, :], [wg, h_])
                    g_ = gb[cnt[0] % 3]
                    t_ = tt_[cnt[0] % 2]
                    cnt[0] += 1
                    P.op("act", lambda e: e.activation(out=g_.ap, in_=pg.ap, func=AF.Sigmoid, bias=bg.ap[:, br * 8 + dch:br * 8 + dch + 1]),
                         reads=[pg, bg], writes=[g_])
                    pb = next_bank()
                    mm_group(pb, 4, lambda c: wb[br].ap[:, c, dch * 128:(dch + 1) * 128], lambda c: yb_.ap[:, br * 4 + c, :], [wb[br], yb_])
                    if br == 0:
                        P.op("dve", lambda e: e.tensor_tensor(out=m_.ap, in0=pb.ap, in1=g_.ap, op=ALU.mult), reads=[pb, g_], writes=[m_])
                    else:
                        P.op("dve", lambda e: e.tensor_tensor(out=t_.ap, in0=pb.ap, in1=g_.ap, op=ALU.mult), reads=[pb, g_], writes=[t_])
                        if br == 1:
                            P.op("pool", lambda e: e.tensor_tensor(out=m_.ap, in0=m_.ap, in1=t_.ap, op=ALU.add), reads=[m_, t_], writes=[m_])
                        else:
                            P.op("pool", lambda e: e.tensor_tensor(out=mg.ap[:, dch, :], in0=m_.ap, in1=t_.ap, op=ALU.add),
                                 reads=[m_, t_], writes=[mg])
                one(br)

        def out_chunk(j, dch):
            xb_ = xblk[j % 2]
            po = next_bank()
            mm_group(po, NCH, lambda c: wo.ap[:, c, dch * 128:(dch + 1) * 128], lambda c: mg.ap[:, c, :], [wo, mg])
            P.op("dve", lambda e: e.tensor_tensor(out=xb_.ap[:, dch, :], in0=po.ap, in1=xb_.ap[:, dch, :], op=ALU.add),
                 reads=[po, xb_], writes=[xb_])

        load_block(0)
        norm_block(0)
        for j in range(NB):
            if j + 1 < NB:
                load_block(j + 1)
            for dch in range(NCH):
                merge_chunk(j, dch)
                if dch == 3 and j + 1 < NB:
                    norm_block(j + 1)
            for dch in range(NCH):
                out_chunk(j, dch)
            xb_ = xblk[j % 2]
            dst = xT[:, :, j * 512:(j + 1) * 512].rearrange("c p t -> p c t")

            def st(xb_, dst, j):
                P.dma("sp", lambda e, s: e.dma_start(out=dst, in_=xb_.ap).then_inc(s, 16), 1, "y%d" % (j % 2), reads=[xb_])
            st(xb_, dst, j)

    def phase_D2(l):
        w1 = sb("E_w1", [NCH, DFF], BF16)
        w2 = sb("E_w2", [DFF // 128, D], BF16)
        gcol = sb("E_g", [NCH], F32)
        xblk = sb("E_x", [NCH, 512], F32)
        sq = sb("E_sq", [NCH, 512], BF16)
        hT = sb("E_h", [NCH, 512], BF16)
        uT = sb("E_u", [DFF // 128, 512], BF16)
        rl = [sb("E_rl%d" % i, [512], F32) for i in range(2)]
        rstd, tmp = rl[0], rl[1]
        xr = [sb("E_xr%d" % i, [512], F32) for i in range(2)]
        load_w_cast(w1, w_ff1[l].rearrange("(c p) n -> p c n", p=128), DFF, "w0")
        load_w_cast(w2, w_ff2[l].rearrange("(c p) n -> p c n", p=128), D, "w1")
        col_load(gcol, norm_mlp[l], NCH, "g")
        bank = [1]

        def next_bank():
            b = bank[0]
            bank[0] = b + 1 if b < 7 else 1
            return ps[b]

        def mm_group(pt, n, lhs_fn, rhs_fn, reads):
            def f(e):
                for c in range(n):
                    ins = e.matmul(pt.ap, lhsT=lhs_fn(c), rhs=rhs_fn(c), start=(c == 0), stop=(c == n - 1))
                return ins
            P.op("pe", f, reads=reads, writes=[pt])

        def load_block(j):
            src = xT[:, :, j * 512:(j + 1) * 512].rearrange("c p t -> p c t")
            P.dma("sp", lambda e, s: e.dma_start(out=xblk.ap, in_=src).then_inc(s, 16), 1, "x0", writes=[xblk])

        def ff1_chunk(j, fc):
            pt = next_bank()
            mm_group(pt, NCH, lambda c: w1.ap[:, c, fc * 128:(fc + 1) * 128], lambda c: hT.ap[:, c, :], [w1, hT])
            r_ = rl[fc % 2]
            P.op("act", lambda e: e.activation(out=r_.ap, in_=pt.ap, func=AF.Relu), reads=[pt], writes=[r_])
            P.op("pool", lambda e: e.tensor_tensor(out=uT.ap[:, fc, :], in0=r_.ap, in1=r_.ap, op=ALU.mult), reads=[r_], writes=[uT])

        def ff2_chunk(j, dch):
            pt = next_bank()
            mm_group(pt, DFF // 128, lambda c: w2.ap[:, c, dch * 128:(dch + 1) * 128], lambda c: uT.ap[:, c, :], [w2, uT])
            x_ = xr[dch % 2]
            o_ = x_
            P.dma("sp", lambda e, s: e.dma_start(out=x_.ap, in_=xT[dch, :, j * 512:(j + 1) * 512]).then_inc(s, 16), 1,
                  "cS%d" % (dch % 2), writes=[x_])
            P.op("dve", lambda e: e.tensor_tensor(out=o_.ap, in0=pt.ap, in1=x_.ap, op=ALU.add), reads=[pt, x_], writes=[o_])
            P.dma("sp", lambda e, s: e.dma_start(out=xT[dch, :, j * 512:(j + 1) * 512], in_=o_.ap).then_inc(s, 16), 1,
                  "y%d" % (dch % 2), reads=[o_])

        load_block(0)
        emit_norm_block(0, xblk, sq, hT, gcol, rstd, tmp, ps[0])
        for j in range(NB):
            if j + 1 < NB:
                load_block(j + 1)
            for fc in range(DFF // 128):
                ff1_chunk(j, fc)
            for dch in range(NCH):
                ff2_chunk(j, dch)
                if dch == 0 and j + 1 < NB:
                    emit_norm_block(j + 1, xblk, sq, hT, gcol, rstd, tmp, ps[0])

    def phase_F():
        gcol = sb("F_g", [NCH], F32)
        xblk = [sb("F_x%d" % i, [NCH, 512], F32) for i in range(2)]
        sq = sb("F_sq", [NCH, 512], BF16)
        hF = sb("F_h", [NCH, 512], F32)
        rstd = sb("F_rstd", [512], F32)
        tmp = sb("F_tmp", [512], F32)
        ot = [sb("F_o%d" % i, [D], F32) for i in range(2)]
        col_load(gcol, final_norm, NCH, "g")
        bank = [1]

        def next_bank():
            b = bank[0]
            bank[0] = b + 1 if b < 7 else 1
            return ps[b]

        def load_block(j):
            xb_ = xblk[j % 2]
            src = xT[:, :, j * 512:(j + 1) * 512].rearrange("c p t -> p c t")
            P.dma("sp", lambda e, s: e.dma_start(out=xb_.ap, in_=src).then_inc(s, 16), 1, "x%d" % (j % 2), writes=[xb_])
        load_block(0)
        k = [0]
        for j in range(NB):
            if j + 1 < NB:
                load_block(j + 1)
            emit_norm_block(j, xblk[j % 2], sq, hF, gcol, rstd, tmp, ps[0])
            for tt in range(4):
                def one(j, tt):
                    o_ = ot[k[0] % 2]
                    k[0] += 1
                    for half in range(2):
                        def hh(half):
                            pt = next_bank()

                            def f(e):
                                for cc in range(4):
                                    c = half * 4 + cc
                                    ins = e.transpose(out=pt.ap[:, cc * 128:(cc + 1) * 128], in_=hF.ap[:, c, tt * 128:(tt + 1) * 128],
                                                      identity=ident.ap)
                                return ins
                            P.op("pe", f, reads=[hF, ident], writes=[pt])
                            if half == 0:
                                P.op("act", lambda e: e.activation(out=o_.ap[:, 0:512], in_=pt.ap, func=AF.Copy), reads=[pt], writes=[o_])
                            else:
                                P.op("dve", lambda e: e.tensor_copy(out=o_.ap[:, 512:1024], in_=pt.ap), reads=[pt], writes=[o_])
                        hh(half)
                    dst = out[j * 512 + tt * 128:j * 512 + (tt + 1) * 128, :]
                    P.dma("sp", lambda e, s: e.dma_start(out=dst, in_=o_.ap).then_inc(s, 16), 1, "y%d" % ((k[0] - 1) % 2), reads=[o_])
                one(j, tt)

    only = stop_after if isinstance(stop_after, (list, tuple)) else None
    if only is not None:
        stop_after = None

    def want(ph):
        return only is None or ph in only
    phase_pre()
    P.barrier()
    for l in range(n_layers):
        done = False
        for nm, fn in (("A", phase_A), ("C", phase_C), ("B", phase_B), ("M", phase_M), ("D1", phase_D1), ("D2", phase_D2)):
            if want(nm):
                arena.reset(persist_mark)
                fn(l)
                P.barrier()
            if stop_after == nm:
                done = True
                break
        if done:
            break
    if stop_after is None and want("F"):
        arena.reset(persist_mark)
        phase_F()

    P.barrier()
    P.emit()
    return nc


def host_constants(S):
    NB = S // 512
    ident = np.eye(128, dtype=np.float32)
    BT = np.zeros((128, 128), np.float32)
    for k in range(8 * NB):
        for m in range(8 * NB):
            if k // NB == m // NB and k % NB < m % NB:
                BT[k, m] = 1.0
    inv_freq = np.power(np.float32(500000.0), -np.arange(0, 16, 2, dtype=np.float32) / np.float32(16)).astype(np.float32)
    rope = np.zeros((128, 2), np.float32)
    for r in range(128):
        rr = r % 64
        if rr < 16:
            rope[r, 0] = inv_freq[rr % 8]
            rope[r, 1] = -1.0 if rr < 8 else 1.0
        else:
            rope[r, 0] = 0.0
            rope[r, 1] = 1.0
    maskC = np.zeros((4, 128, 512), np.float32)
    maskB = np.zeros((4, 128, 512), np.float32)
    k = np.arange(128)[:, None]
    q = np.arange(512)[None, :]
    for i in range(4):
        kk = 128 * i + k
        maskC[i] = np.where(kk <= q, 0.0, NEG)
        maskB[i] = np.where((kk // 64) <= (q // 64), 0.0, NEG)
    return dict(c_ident=ident, c_rope=rope, c_maskC=maskC, c_maskB=maskB, c_BT=BT)


def host_biasA(rel_bias):
    L = rel_bias.shape[0]
    k = np.arange(128)[:, None]
    q = np.arange(128)[None, :]
    tab = np.empty((L, 5, 128, 8, 128), np.float32)
    for i in range(5):
        koff = 128 * (i - 4) + k
        rel = np.clip(koff - q, -128, 128) + 128
        qc = q // 64
        valid = (koff >= 64 * qc - 512) & (koff < 64 * qc + 64)
        g = rel_bias[:, :, rel]
        g = np.where(valid[None, None], g, np.float32(NEG))
        tab[:, i] = g.transpose(0, 2, 1, 3)
    return tab


_CACHE = {}


def make_in_maps(inputs, S, nb):
    consts = host_constants(S)
    biasA = host_biasA(np.asarray(inputs["rel_bias"], np.float32))
    maps = []
    for b in range(nb):
        m = dict(consts)
        m["x"] = np.ascontiguousarray(inputs["x"][b, :S])
        m["positions"] = np.ascontiguousarray(inputs["positions"][b:b + 1, :S]).astype(np.int32)
        m["biasA"] = biasA
        m["lambdas"] = np.ascontiguousarray(np.asarray(inputs["lambdas"], np.float32).reshape(DEPTH, 1, 256))
        for k in ("norm_mix", "w_in", "sub_gain", "b_forget", "w_br_a", "w_br_b", "w_br_c", "w_gate", "b_gate",
                  "w_out", "norm_mlp", "w_ff1", "w_ff2", "final_norm"):
            m[k] = np.ascontiguousarray(np.asarray(inputs[k], np.float32))
        maps.append(m)
    return maps


def kernel(**inputs):
    S = inputs["x"].shape[1]
    nb = inputs["x"].shape[0]
    if "nc" not in _CACHE:
        _CACHE["nc"] = build_program(S)
    nc = _CACHE["nc"]
    maps = make_in_maps(inputs, S, nb)
    res = run_bass_kernel_spmd(nc, maps, core_ids=list(range(nb)))
    return np.stack([r["out"] for r in res.results], axis=0).astype(np.float32)
```

```python
import math
from contextlib import ExitStack

import numpy as np
import concourse.bass as bass
import concourse.mybir as mybir
from concourse.bass_utils import run_bass_kernel_spmd

F32 = mybir.dt.float32
BF16 = mybir.dt.bfloat16
I32 = mybir.dt.int32
AF = mybir.ActivationFunctionType
ALU = mybir.AluOpType

D = 1024
DEPTH = 2
NCH = D // 128
INW = 4616
DFF = 4096
EPS = 1e-6
NEG = -1.0e5

ENGS = ("pe", "act", "dve", "pool", "sp")
SP_KEYS = ("const", "g", "bf", "q0", "q1", "k0", "k1", "kk0", "kk1")
SAME_ENGINE_SYNC = True


class Tile:
    __slots__ = ("name", "ap", "w", "r")

    def __init__(self, name, ap):
        self.name = name
        self.ap = ap
        self.w = None
        self.r = []


class Op:
    __slots__ = ("eng", "fn", "deps", "signal", "semval", "is_dma", "sem", "name")

    def __init__(self, eng, fn, name=""):
        self.eng = eng
        self.fn = fn
        self.deps = []
        self.signal = False
        self.semval = 0
        self.is_dma = False
        self.sem = None
        self.name = name


class Prog:
    def __init__(self, nc):
        self.nc = nc
        self.ops = {e: [] for e in ENGS}
        self.dma_keys = {}
        self.tiles = []
        self.n_ops = 0

    def tile(self, name, ap):
        t = Tile(name, ap)
        self.tiles.append(t)
        return t

    def _add_deps(self, op, reads, writes):
        deps = op.deps
        for t in reads:
            if t.w is not None and t.w not in deps:
                deps.append(t.w)
        for t in writes:
            if t.w is not None and t.w not in deps:
                deps.append(t.w)
            for r in t.r:
                if r not in deps and r is not op:
                    deps.append(r)
        for t in reads:
            t.r.append(op)
        for t in writes:
            t.w = op
            t.r = []

    def op(self, eng, fn, reads=(), writes=(), name=""):
        o = Op(eng, fn, name)
        self._add_deps(o, reads, writes)
        self.ops[eng].append(o)
        self.n_ops += 1
        return o

    def dma(self, eng, fn, n, key, reads=(), writes=(), name=""):
        eng = "sp" if key in SP_KEYS else "pool"
        o = Op(eng, fn, name)
        o.is_dma = True
        st = self.dma_keys.setdefault(key, [0, None])
        if st[1] is not None:
            o.deps.append(st[1])
        st[0] += n
        st[1] = o
        o.sem = key
        o.semval = 16 * st[0]
        self._add_deps(o, reads, writes)
        self.ops[eng].append(o)
        self.n_ops += 1
        return o

    def barrier(self):
        lasts = []
        for e in ENGS:
            for o in reversed(self.ops[e]):
                if o.fn is not None and not o.is_dma:
                    lasts.append(o)
                    break
        for k, st in self.dma_keys.items():
            if st[1] is not None:
                lasts.append(st[1])
        for e in ENGS:
            o = Op(e, None, "barrier")
            o.deps = [l for l in lasts if (l.is_dma or l.eng != e)]
            self.ops[e].append(o)
        for t in self.tiles:
            t.w = None
            t.r = []

    def emit(self):
        nc = self.nc
        for e in ENGS:
            for o in self.ops[e]:
                for d in o.deps:
                    if not d.is_dma:
                        if d.eng == o.eng and not SAME_ENGINE_SYNC:
                            continue
                        d.signal = True
        for e in ENGS:
            c = 0
            for o in self.ops[e]:
                if o.is_dma or o.fn is None:
                    continue
                if o.signal:
                    c += 1
                    o.semval = c
        with ExitStack() as es:
            esem = {e: es.enter_context(nc.semaphore("s_" + e)) for e in ENGS}
            dsem = {k: es.enter_context(nc.semaphore("d_%s" % (k,))) for k in self.dma_keys}
            block = es.enter_context(nc.Block())

            def run(e, eng):
                known = {}
                for o in self.ops[e]:
                    for d in o.deps:
                        if d.is_dma:
                            sem = dsem[d.sem]
                        else:
                            if d.eng == e and not SAME_ENGINE_SYNC:
                                continue
                            sem = esem[d.eng]
                        if known.get(sem.num, 0) >= d.semval:
                            continue
                        known[sem.num] = d.semval
                        eng.wait_ge(sem, d.semval)
                    if o.fn is None:
                        continue
                    if o.is_dma:
                        o.fn(eng, dsem[o.sem])
                    else:
                        ins = o.fn(eng)
                        if o.signal:
                            ins.then_inc(esem[e], 1)

            @block.tensor
            def _(eng):
                run("pe", eng)

            @block.scalar
            def _(eng):
                run("act", eng)

            @block.vector
            def _(eng):
                run("dve", eng)

            @block.gpsimd
            def _(eng):
                run("pool", eng)

            @block.sync
            def _(eng):
                run("sp", eng)


class Arena:
    def __init__(self, nc, nbytes):
        self.t = nc.alloc_sbuf_tensor("arena", [128, nbytes // 4], F32)
        self.cap = nbytes
        self.off = 0

    def mark(self):
        return self.off

    def reset(self, m=0):
        self.off = m

    def alloc(self, shape, dtype):
        esz = 2 if dtype == BF16 else 4
        n = 1
        for s in shape:
            n *= s
        nb = (n * esz + 31) // 32 * 32
        assert self.off + nb <= self.cap, ("SBUF arena overflow", self.off, nb, self.cap)
        a = self.t[:, self.off // 4:(self.off + nb) // 4]
        self.off += nb
        if dtype != F32:
            a = a.bitcast(dtype)
        a = a[:, 0:n]
        if len(shape) == 2:
            a = a.rearrange("p (a b) -> p a b", a=shape[0])
        elif len(shape) == 3:
            a = a.rearrange("p (a b c) -> p a b c", a=shape[0], b=shape[1])
        return a


def build_program(S, debug=False, n_layers=DEPTH, stop_after=None):
    assert S % 512 == 0
    NB = S // 512
    NT = S // 128
    nc = bass.Bass("TRN2", target_bir_lowering=False)
    P = Prog(nc)
    ext_out = "ExternalOutput" if debug else "Internal"

    def din(name, shape, dt=F32):
        return nc.dram_tensor(name, list(shape), dt, kind="ExternalInput").ap()

    x_in = din("x", [S, D])
    pos_in = din("positions", [1, S], I32)
    norm_mix = din("norm_mix", [DEPTH, D])
    w_in = din("w_in", [DEPTH, D, INW])
    biasA = din("biasA", [DEPTH, 5, 128, 8, 128])
    lambdas = din("lambdas", [DEPTH, 1, 256])
    sub_gain = din("sub_gain", [DEPTH, 128])
    b_forget = din("b_forget", [DEPTH, 8])
    w_br = [din("w_br_a", [DEPTH, 512, D]), din("w_br_b", [DEPTH, 512, D]), din("w_br_c", [DEPTH, 512, D])]
    w_gate = din("w_gate", [DEPTH, D, 3 * D])
    b_gate = din("b_gate", [DEPTH, 3 * D])
    w_out = din("w_out", [DEPTH, D, D])
    norm_mlp = din("norm_mlp", [DEPTH, D])
    w_ff1 = din("w_ff1", [DEPTH, D, DFF])
    w_ff2 = din("w_ff2", [DEPTH, DFF, D])
    final_norm = din("final_norm", [D])
    c_ident = din("c_ident", [128, 128])
    c_rope = din("c_rope", [128, 2])
    c_maskC = din("c_maskC", [4, 128, 512])
    c_maskB = din("c_maskB", [4, 128, 512])
    c_BT = din("c_BT", [128, 128])

    out = nc.dram_tensor("out", [S, D], F32, kind="ExternalOutput").ap()

    def scr(name, shape, dt, dbg=True):
        return nc.dram_tensor(name, list(shape), dt, kind=(ext_out if dbg else "Internal")).ap()

    xT = scr("xT", [NCH, 128, S], F32)
    ropeC = scr("ropeC", [128, S], F32)
    ropeS = scr("ropeS", [128, S], F32)
    QT = {m: scr("QT" + m, [512, S], BF16) for m in "abc"}
    KT = {m: scr("KT" + m, [512, S], BF16) for m in "abc"}
    V = {m: scr("V" + m, [S, 512], BF16) for m in "abc"}
    ls_scr = scr("ls", [8, S], F32)
    Gq = scr("Gq", [8, S], BF16)
    Ah = scr("Ah", [8, S], BF16)
    Al = scr("Al", [8, S], BF16)
    yT = scr("yT", [1536, S], BF16)

    T_xT = P.tile("xT", xT)
    T_rope = P.tile("rope", ropeC)
    T_qkv = P.tile("qkv", None)
    T_ls = P.tile("ls", None)
    T_Gq = P.tile("Gq", None)
    T_yT = P.tile("yT", None)
    T_out = P.tile("out", None)

    arena = Arena(nc, 207 * 1024)
    ps = [P.tile("ps%d" % i, nc.alloc_psum_tensor("ps%d" % i, [128, 512], F32)[:, :]) for i in range(8)]

    def sb(name, shape, dt):
        return P.tile(name, arena.alloc(shape, dt))

    ident = sb("ident", [128], F32)
    ones_bf = sb("ones_bf", [128], BF16)
    ones_f = sb("ones_f", [128], F32)
    ropec = sb("ropec", [2], F32)
    P.dma("sp", lambda e, s: e.dma_start(out=ident.ap, in_=c_ident).then_inc(s, 16), 1, "const", writes=[ident])
    P.dma("sp", lambda e, s: e.dma_start(out=ropec.ap, in_=c_rope).then_inc(s, 16), 1, "const", writes=[ropec])
    P.op("dve", lambda e: e.memset(ones_bf.ap, 1.0), writes=[ones_bf])
    P.op("dve", lambda e: e.memset(ones_f.ap, 1.0), writes=[ones_f])
    persist_mark = arena.mark()

    def phase_pre():
        xb = [sb("pre_x%d" % i, [4, D], F32) for i in range(2)]
        xt = [sb("pre_xt%d" % i, [NCH, 512], F32) for i in range(2)]
        for j in range(NB):
            xi, xo = xb[j % 2], xt[j % 2]
            src = x_in[j * 512:(j + 1) * 512, :].rearrange("(t p) d -> p t d", p=128)
            P.dma("sp", lambda e, s, xi=xi, src=src: e.dma_start(out=xi.ap, in_=src).then_inc(s, 16), 1,
                  "pre_x%d" % (j % 2), writes=[xi])
            for c in range(NCH):
                pt = ps[c % 4]

                def f(e, xi=xi, pt=pt, c=c):
                    for tt in range(4):
                        ins = e.transpose(out=pt.ap[:, tt * 128:(tt + 1) * 128],
                                          in_=xi.ap[:, tt, c * 128:(c + 1) * 128], identity=ident.ap)
                    return ins
                P.op("pe", f, reads=[xi, ident], writes=[pt])
                if c % 2 == 0:
                    P.op("dve", lambda e, xo=xo, pt=pt, c=c: e.tensor_copy(out=xo.ap[:, c, :], in_=pt.ap),
                         reads=[pt], writes=[xo])
                else:
                    P.op("act", lambda e, xo=xo, pt=pt, c=c: e.activation(out=xo.ap[:, c, :], in_=pt.ap, func=AF.Copy),
                         reads=[pt], writes=[xo])
            dst = xT[:, :, j * 512:(j + 1) * 512].rearrange("c p t -> p c t")
            P.dma("pool", lambda e, s, xo=xo, dst=dst: e.dma_start(out=dst, in_=xo.ap).then_inc(s, 16), 1,
                  "pre_xt%d" % (j % 2), reads=[xo])
        RB = min(2048, S)
        posi = sb("pre_posi", [RB], I32)
        ang = sb("pre_ang", [RB], F32)
        t1 = sb("pre_t1", [RB], F32)
        ni = sb("pre_ni", [RB], I32)
        tb = sb("pre_tb", [RB], F32)
        TWO_PI = 2.0 * math.pi
        for j in range(S // RB):
            src = pos_in[0:1, j * RB:(j + 1) * RB].partition_broadcast(128)
            P.dma("sp", lambda e, s, src=src: e.dma_start(out=posi.ap.unsqueeze(1), in_=src).then_inc(s, 16), 1,
                  "pre_pos", writes=[posi])
            P.op("dve", lambda e: e.tensor_copy(out=ang.ap, in_=posi.ap), reads=[posi], writes=[ang])
            P.op("dve", lambda e: e.tensor_scalar(out=ang.ap, in0=ang.ap, scalar1=ropec.ap[:, 0:1], scalar2=None,
                                                  op0=ALU.mult), reads=[ang, ropec], writes=[ang])
            for which, shift, dstT in (("s", 0.0, ropeS), ("c", 0.5 * math.pi, ropeC)):
                P.op("dve", lambda e, shift=shift: e.tensor_scalar(out=t1.ap, in0=ang.ap, scalar1=shift, scalar2=1.0 / TWO_PI,
                                                                   op0=ALU.add, op1=ALU.mult), reads=[ang], writes=[t1])
                P.op("dve", lambda e: e.tensor_copy(out=ni.ap, in_=t1.ap), reads=[t1], writes=[ni])
                P.op("dve", lambda e: e.tensor_copy(out=t1.ap, in_=ni.ap), reads=[ni], writes=[t1])
                P.op("dve", lambda e: e.tensor_scalar(out=t1.ap, in0=t1.ap, scalar1=-TWO_PI, scalar2=None, op0=ALU.mult),
                     reads=[t1], writes=[t1])
                P.op("dve", lambda e, shift=shift: e.scalar_tensor_tensor(out=tb.ap, in0=ang.ap, scalar=shift, in1=t1.ap,
                                                                          op0=ALU.add, op1=ALU.add),
                     reads=[ang, t1], writes=[tb])
                P.op("dve", lambda e: e.tensor_scalar(out=t1.ap, in0=tb.ap, scalar1=math.pi, scalar2=-TWO_PI,
                                                      op0=ALU.is_gt, op1=ALU.mult), reads=[tb], writes=[t1])
                P.op("dve", lambda e: e.tensor_tensor(out=tb.ap, in0=tb.ap, in1=t1.ap, op=ALU.add),
                     reads=[tb, t1], writes=[tb])
                P.op("dve", lambda e: e.tensor_scalar(out=t1.ap, in0=tb.ap, scalar1=-math.pi, scalar2=TWO_PI,
                                                      op0=ALU.is_lt, op1=ALU.mult), reads=[tb], writes=[t1])
                P.op("dve", lambda e: e.tensor_tensor(out=tb.ap, in0=tb.ap, in1=t1.ap, op=ALU.add),
                     reads=[tb, t1], writes=[tb])
                P.op("dve", lambda e: e.tensor_scalar(out=tb.ap, in0=tb.ap, scalar1=-math.pi, scalar2=math.pi,
                                                      op0=ALU.max, op1=ALU.min), reads=[tb], writes=[tb])
                if which == "s":
                    P.op("act", lambda e: e.activation(out=tb.ap, in_=tb.ap, func=AF.Sin, scale=ropec.ap[:, 1:2]),
                         reads=[tb, ropec], writes=[tb])
                else:
                    P.op("act", lambda e: e.activation(out=tb.ap, in_=tb.ap, func=AF.Sin), reads=[tb], writes=[tb])
                dst = dstT[:, j * RB:(j + 1) * RB]
                P.dma("sp", lambda e, s, dst=dst: e.dma_start(out=dst, in_=tb.ap).then_inc(s, 16), 1,
                      "pre_tb", reads=[tb])

    def emit_norm_block(j, xblk, sq, hT, gcol, rstd, tmp, ps_ssq):
        P.op("pool", lambda e: e.tensor_tensor(out=sq.ap, in0=xblk.ap, in1=xblk.ap, op=ALU.mult),
             reads=[xblk], writes=[sq])

        def f(e):
            for c in range(NCH):
                ins = e.matmul(ps_ssq.ap, lhsT=ones_bf.ap, rhs=sq.ap[:, c, :], start=(c == 0), stop=(c == NCH - 1))
            return ins
        P.op("pe", f, reads=[sq, ones_bf], writes=[ps_ssq])
        P.op("act", lambda e: e.activation(out=tmp.ap, in_=ps_ssq.ap, func=AF.Ln, scale=1.0 / D, bias=epsc.ap),
             reads=[ps_ssq, epsc], writes=[tmp])
        P.op("act", lambda e: e.activation(out=rstd.ap, in_=tmp.ap, func=AF.Exp, scale=-0.5),
             reads=[tmp], writes=[rstd])
        for c in range(NCH):
            P.op("dve", lambda e, c=c: e.scalar_tensor_tensor(out=hT.ap[:, c, :], in0=xblk.ap[:, c, :],
                                                              scalar=gcol.ap[:, c:c + 1], in1=rstd.ap,
                                                              op0=ALU.mult, op1=ALU.mult),
                 reads=[xblk, gcol, rstd], writes=[hT])

    epsc = sb("epsc", [1], F32)
    P.op("dve", lambda e: e.memset(epsc.ap, EPS), writes=[epsc])
    persist_mark = arena.mark()

    def phase_A(l):
        w = sb("A_w", [NCH, INW], BF16)
        wsw = sb("w1", [NCH, 1024], BF16)
        gcol = sb("A_g", [NCH], F32)
        bfc = sb("A_bf", [1], F32)
        xblk = [sb("A_x%d" % i, [NCH, 512], F32) for i in range(2)]
        sq = sb("A_sq", [NCH, 512], BF16)
        hT = [sb("A_h%d" % i, [NCH, 512], BF16) for i in range(2)]
        rstd = sb("A_rstd", [512], F32)
        tmp = sb("A_tmp", [512], F32)
        cS = [sb("A_cS%d" % i, [512], F32) for i in range(2)]
        cC = [sb("A_cC%d" % i, [512], F32) for i in range(2)]
        stg = [sb("A_stg%d" % i, [512], BF16) for i in range(4)]
        r1 = [sb("A_r1_%d" % i, [512], F32) for i in range(2)]
        r2 = [sb("A_r2_%d" % i, [512], F32) for i in range(2)]
        fl1 = sb("A_fl1", [512], F32)
        fl2 = sb("A_fl2", [512], F32)

        wsrc = w_in[l].rearrange("(c p) n -> p c n", p=128)
        def fw(e, s):
            for c in range(NCH):
                for a, b in ((0, 2048), (2048, 4096), (4096, INW)):
                    e.dma_start(out=w.ap[:, c, a:b], in_=wsrc[:, c, a:b]).then_inc(s, 16)
        P.dma("pool", fw, 3 * NCH, "w0", writes=[w])
        P.op("pool", lambda e: e.memset(wsw.ap, 0.0), writes=[wsw])
        qb0 = 1536
        srcg = w.ap[:, :, qb0:qb0 + 1024].rearrange("p c (g k) -> p c g k", k=64)
        dstg = wsw.ap.rearrange("p c (g k) -> p c g k", k=64)
        P.op("pool", lambda e: e.tensor_copy(out=dstg[:, :, :, 8:16], in_=srcg[:, :, :, 0:8]), reads=[w, wsw], writes=[wsw])
        P.op("pool", lambda e: e.tensor_copy(out=dstg[:, :, :, 0:8], in_=srcg[:, :, :, 8:16]), reads=[w, wsw], writes=[wsw])
        def fg(e, s):
            with nc.allow_non_contiguous_dma(reason="small"):
                e.dma_start(out=gcol.ap, in_=norm_mix[l].rearrange("(c p) -> p c", p=128)).then_inc(s, 16)
        P.dma("sp", fg, 1, "g", writes=[gcol])
        P.op("dve", lambda e: e.memset(bfc.ap, 0.0), writes=[bfc])
        def fb(e, s):
            with nc.allow_non_contiguous_dma(reason="small"):
                e.dma_start(out=bfc.ap[120:128, :], in_=b_forget[l].unsqueeze(1)).then_inc(s, 16)
        P.dma("sp", fb, 1, "bf", writes=[bfc])
        P.op("dve", lambda e: e.tensor_scalar(out=bfc.ap, in0=bfc.ap, scalar1=-1.0, scalar2=None, op0=ALU.mult),
             reads=[bfc], writes=[bfc])

        bank = [1]

        def next_bank():
            b = bank[0]
            bank[0] = b + 1 if b < 7 else 1
            return ps[b]
        evac = [0]
        stgi = [0]

        def smalldma(eng, out_ap, in_ap, key, **kw):
            def f(e, s):
                with nc.allow_non_contiguous_dma(reason="small"):
                    e.dma_start(out=out_ap, in_=in_ap).then_inc(s, 16)
            P.dma(eng, f, 1, key, **kw)

        def load_block(j):
            xb_ = xblk[j % 2]
            src = xT[:, :, j * 512:(j + 1) * 512].rearrange("c p t -> p c t")
            P.dma("sp", lambda e, s: e.dma_start(out=xb_.ap, in_=src).then_inc(s, 16), 1, "x%d" % (j % 2), writes=[xb_])
            cs_, cc_ = cS[j % 2], cC[j % 2]
            P.dma("sp", lambda e, s: e.dma_start(out=cs_.ap, in_=ropeS[:, j * 512:(j + 1) * 512]).then_inc(s, 16),
                  1, "cS%d" % (j % 2), writes=[cs_])
            P.dma("sp", lambda e, s: e.dma_start(out=cc_.ap, in_=ropeC[:, j * 512:(j + 1) * 512]).then_inc(s, 16),
                  1, "cC%d" % (j % 2), writes=[cc_])

        def norm_block(j):
            emit_norm_block(j, xblk[j % 2], sq, hT[j % 2], gcol, rstd, tmp, ps[0])

        def store(st, dst):
            k = stgi[0] % 4
            import os
            hk = os.environ.get("HACK", "")
            if hk == "nostore":
                return
            if hk == "noV" and dst.shape[1] == 512 and dst.shape[0] == 128 and getattr(store, "isv", False):
                return
            if hk == "onlyV" and not getattr(store, "isv", False):
                return
            if hk == "poolstore":
                P.dma("pool", lambda e, s: e.dma_start(out=dst, in_=st.ap).then_inc(s, 16), 1, "stg%d" % k, reads=[st])
                return
            if hk == "contig":
                if "dummy" not in _CACHE:
                    _CACHE["dummy"] = nc.dram_tensor("dummyx", [4, 128, 512], BF16).ap()
                dd = _CACHE["dummy"][k]
                P.dma("sp", lambda e, s: e.dma_start(out=dd, in_=st.ap).then_inc(s, 16), 1, "stg%d" % k, reads=[st])
                return
            if hk == "half":
                P.dma("sp", lambda e, s: e.dma_start(out=dst[0:64, :], in_=st.ap[0:64, :]).then_inc(s, 16), 1, "stg%d" % k, reads=[st])
                return
            if hk == "tiny":
                P.dma("sp", lambda e, s: e.dma_start(out=dst[0:1, :], in_=st.ap[0:1, :]).then_inc(s, 16), 1, "stg%d" % k, reads=[st])
                return
            P.dma("sp", lambda e, s: e.dma_start(out=dst, in_=st.ap).then_inc(s, 16), 1, "stg%d" % k, reads=[st])

        def mm_group(pt, lhs_fn, rhs_fn, reads):
            def f(e):
                for c in range(NCH):
                    ins = e.matmul(pt.ap, lhsT=lhs_fn(c), rhs=rhs_fn(c), start=(c == 0), stop=(c == NCH - 1))
                return ins
            P.op("pe", f, reads=reads, writes=[pt])

        def plain_evac(st, pt):
            if evac[0] % 2 == 0:
                P.op("act", lambda e: e.activation(out=st.ap, in_=pt.ap, func=AF.Copy), reads=[pt], writes=[st])
            else:
                P.op("dve", lambda e: e.tensor_copy(out=st.ap, in_=pt.ap), reads=[pt], writes=[st])
            evac[0] += 1

        def proj_fm(j, nm, dstT, col0):
            h_ = hT[j % 2]
            cs_, cc_ = cS[j % 2], cC[j % 2]
            is_b = nm.endswith("b")

            def one(m):
                pt = next_bank()
                cols = slice(col0 + m * 128, col0 + (m + 1) * 128)
                mm_group(pt, lambda c: w.ap[:, c, cols], lambda c: h_.ap[:, c, :], [w, h_])
                st = stg[stgi[0] % 4]
                if not is_b:
                    plain_evac(st, pt)
                else:
                    pt2 = next_bank()
                    sc = slice((col0 - qb0) + m * 128, (col0 - qb0) + (m + 1) * 128)
                    mm_group(pt2, lambda c: wsw.ap[:, c, sc], lambda c: h_.ap[:, c, :], [wsw, h_])
                    a1, a2 = r1[evac[0] % 2], r2[evac[0] % 2]
                    evac[0] += 1
                    P.op("dve", lambda e: e.tensor_tensor(out=a1.ap, in0=pt.ap, in1=cc_.ap, op=ALU.mult),
                         reads=[pt, cc_], writes=[a1])
                    P.op("dve", lambda e: e.tensor_tensor(out=a2.ap, in0=pt2.ap, in1=cs_.ap, op=ALU.mult),
                         reads=[pt2, cs_], writes=[a2])
                    P.op("pool", lambda e: e.tensor_tensor(out=st.ap, in0=a1.ap, in1=a2.ap, op=ALU.add),
                         reads=[a1, a2], writes=[st])
                store(st, dstT[m * 128:(m + 1) * 128, j * 512:(j + 1) * 512])
                stgi[0] += 1
            for m in range(4):
                one(m)

        def proj_v(j, dstV, col0):
            h_ = hT[j % 2]

            def one(tt):
                pt = next_bank()
                mm_group(pt, lambda c: h_.ap[:, c, tt * 128:(tt + 1) * 128], lambda c: w.ap[:, c, col0:col0 + 512], [w, h_])
                st = stg[stgi[0] % 4]
                plain_evac(st, pt)
                store.isv = True
                store(st, dstV[j * 512 + tt * 128: j * 512 + (tt + 1) * 128, :])
                store.isv = False
                stgi[0] += 1
            for tt in range(4):
                one(tt)

        def proj_fl(j):
            h_ = hT[j % 2]
            pt = next_bank()
            mm_group(pt, lambda c: w.ap[:, c, INW - 128:INW], lambda c: h_.ap[:, c, :], [w, h_])
            P.op("act", lambda e: e.activation(out=fl1.ap, in_=pt.ap, func=AF.Exp, scale=-1.0, bias=bfc.ap),
                 reads=[pt, bfc], writes=[fl1])
            P.op("act", lambda e: e.activation(out=fl2.ap, in_=fl1.ap, func=AF.Ln, scale=1.0, bias=ones_f.ap[:, 0:1]),
                 reads=[fl1, ones_f], writes=[fl2])
            P.op("dve", lambda e: e.tensor_scalar(out=fl1.ap, in0=fl2.ap, scalar1=-1.0, scalar2=None, op0=ALU.mult),
                 reads=[fl2], writes=[fl1])
            P.dma("sp", lambda e, s: e.dma_start(out=ls_scr[:, j * 512:(j + 1) * 512], in_=fl1.ap[120:128, :]).then_inc(s, 16),
                  1, "fl", reads=[fl1])

        import os
        NBL = int(os.environ.get("NBLIM", NB))
        load_block(0)
        norm_block(0)
        for j in range(NBL):
            if j + 1 < NB:
                load_block(j + 1)
            proj_fm(j, "QTa", QT["a"], 0)
            proj_fm(j, "KTa", KT["a"], 512)
            proj_fm(j, "QTb", QT["b"], 1536)
            proj_fm(j, "KTb", KT["b"], 2048)
            if j + 1 < NB:
                norm_block(j + 1)
            proj_fm(j, "QTc", QT["c"], 3072)
            proj_fm(j, "KTc", KT["c"], 3584)
            proj_v(j, V["a"], 1024)
            proj_v(j, V["b"], 2560)
            proj_v(j, V["c"], 4096)
            proj_fl(j)

    def phase_C(l):
        PP = 8 * NB
        Lt = sb("B0_L", [512], F32)
        Fl = sb("B0_F", [512], F32)
        one5 = sb("B0_one", [512], F32)
        G8 = sb("B0_G8", [512], BF16)
        offs = sb("B0_off", [1], F32)
        dR = sb("B0_dR", [128], F32)
        BT = sb("B0_BT", [128], F32)
        Fcol2 = sb("C_Fcol", [8 * NT], F32)
        Rbc = sb("C_Rbc", [PP], F32)
        sel = sb("C_sel", [64], F32)
        P.dma("sp", lambda e, s: e.dma_start(out=BT.ap, in_=c_BT).then_inc(s, 16), 1, "g", writes=[BT])
        P.op("dve", lambda e: e.memset(one5.ap, 1.0), writes=[one5])
        P.op("dve", lambda e: e.memset(sel.ap, 0.0), writes=[sel])
        P.op("dve", lambda e: e.memset(sel.ap[64:65, :], 1.0), reads=[sel], writes=[sel])
        P.dma("sp", lambda e, s: e.dma_start(out=Lt.ap[0:PP, :], in_=ls_scr.rearrange("h (s t) -> (h s) t", t=512)).then_inc(s, 16),
              1, "x0", writes=[Lt])
        P.op("dve", lambda e: e.tensor_tensor_scan(out=Fl.ap[0:PP, :], data0=one5.ap[0:PP, :], data1=Lt.ap[0:PP, :], initial=0.0,
                                                   op0=ALU.mult, op1=ALU.add), reads=[one5, Lt], writes=[Fl])
        P.op("pe", lambda e: e.matmul(ps[0].ap[0:PP, 0:1], lhsT=BT.ap[0:PP, 0:PP], rhs=Fl.ap[0:PP, 511:512], start=True, stop=True),
             reads=[BT, Fl], writes=[ps[0]])
        P.op("dve", lambda e: e.tensor_copy(out=offs.ap[0:PP, :], in_=ps[0].ap[0:PP, 0:1]), reads=[ps[0]], writes=[offs])
        P.op("dve", lambda e: e.tensor_scalar(out=Fl.ap[0:PP, :], in0=Fl.ap[0:PP, :], scalar1=offs.ap[0:PP, 0:1], scalar2=None,
                                              op0=ALU.add), reads=[Fl, offs], writes=[Fl])
        ahi = sb("B0_ahi", [512], BF16)
        alo = sb("B0_alo", [512], BF16)
        P.op("dve", lambda e: e.tensor_scalar(out=G8.ap[0:PP, :], in0=Fl.ap[0:PP, :], scalar1=8.0, scalar2=None, op0=ALU.mult),
             reads=[Fl], writes=[G8])
        P.op("dve", lambda e: e.tensor_scalar(out=ahi.ap[0:PP, :], in0=Fl.ap[0:PP, :], scalar1=-8.0, scalar2=None, op0=ALU.mult),
             reads=[Fl], writes=[ahi])
        P.op("dve", lambda e: e.scalar_tensor_tensor(out=alo.ap[0:PP, :], in0=Fl.ap[0:PP, :], scalar=-8.0, in1=ahi.ap[0:PP, :],
                                                     op0=ALU.mult, op1=ALU.subtract), reads=[Fl, ahi], writes=[alo])
        P.dma("sp", lambda e, s: e.dma_start(out=Gq.rearrange("h (s t) -> (h s) t", t=512), in_=G8.ap[0:PP, :]).then_inc(s, 16),
              1, "x1", reads=[G8])
        P.dma("sp", lambda e, s: e.dma_start(out=Ah.rearrange("h (s t) -> (h s) t", t=512), in_=ahi.ap[0:PP, :]).then_inc(s, 16),
              1, "x1", reads=[ahi])
        P.dma("sp", lambda e, s: e.dma_start(out=Al.rearrange("h (s t) -> (h s) t", t=512), in_=alo.ap[0:PP, :]).then_inc(s, 16),
              1, "x1", reads=[alo])
        gq_store = P.dma_keys["x1"][1]

        Vaug = sb("C_V", [NT, 8, 65], BF16)
        VC = min(8, NT)
        Vch = [P.tile("C_Vch%d" % i, Vaug.ap) for i in range(NT // VC)]
        vst = sb("C_vst", [VC, 512], BF16)
        P.op("pool", lambda e: e.memset(Vaug.ap[:, :, :, 64:65], 1.0), writes=Vch)
        for t0 in range(0, NT, VC):
            def fv(t0):
                src = V["c"][t0 * 128:(t0 + VC) * 128, :].rearrange("(t p) c -> p t c", p=128)
                P.dma("sp", lambda e, s: e.dma_start(out=vst.ap, in_=src).then_inc(s, 16), 1, "x0", writes=[vst])
                P.op("pool", lambda e: e.tensor_copy(out=Vaug.ap[:, t0:t0 + VC, :, 0:64],
                                                     in_=vst.ap.rearrange("p t (h d) -> p t h d", h=8)),
                     reads=[vst, Vch[t0 // VC]], writes=[Vch[t0 // VC]])
            fv(t0)
        mk = sb("C_mask", [4, 512], F32)
        P.dma("sp", lambda e, s: e.dma_start(out=mk.ap, in_=c_maskC.rearrange("i k q -> k i q")).then_inc(s, 16), 1, "g", writes=[mk])

        QTh = [sb("C_Q%d" % i, [S], BF16) for i in range(2)]
        KTh = [sb("C_K%d" % i, [S], BF16) for i in range(2)]
        for i in range(2):
            P.op("pool", lambda e, i=i: e.memset(KTh[i].ap[64:128, :], 0.0), writes=[KTh[i]])
            P.op("pool", lambda e, i=i: e.memset(KTh[i].ap[64:65, :], 1.0), reads=[KTh[i]], writes=[KTh[i]])
            P.op("pool", lambda e, i=i: e.memset(QTh[i].ap[64:128, :], 1.0), writes=[QTh[i]])
        pT = [sb("C_p%d" % i, [512], BF16) for i in range(4)]
        dtmp = [sb("C_dt%d" % i, [512], F32) for i in range(2)]
        bcol = [sb("C_bc%d" % i, [NT], F32) for i in range(2)]
        osb = [sb("C_o%d" % i, [512], F32) for i in range(2)]
        rzb = [sb("C_rz%d" % i, [512], F32) for i in range(2)]
        yst = [sb("C_y%d" % i, [512], BF16) for i in range(2)]

        def load_head(h):
            q_, k_ = QTh[h % 2], KTh[h % 2]

            def fq(e, s):
                e.dma_start(out=q_.ap[0:64, :], in_=QT["c"][h * 64:(h + 1) * 64, :]).then_inc(s, 16)
                e.dma_start(out=q_.ap[64:65, :], in_=Gq[h:h + 1, :]).then_inc(s, 16)
            o = P.dma("sp", fq, 2, "q%d" % (h % 2), reads=[q_], writes=[q_])
            if gq_store not in o.deps:
                o.deps.append(gq_store)
            def fk(e, s):
                e.dma_start(out=k_.ap[0:64, :], in_=KT["c"][h * 64:(h + 1) * 64, :]).then_inc(s, 16)
                e.dma_start(out=k_.ap[65:66, :], in_=Ah[h:h + 1, :]).then_inc(s, 16)
                e.dma_start(out=k_.ap[66:67, :], in_=Al[h:h + 1, :]).then_inc(s, 16)
            o2 = P.dma("sp", fk, 3, "k%d" % (h % 2), reads=[k_], writes=[k_])
            if gq_store not in o2.deps:
                o2.deps.append(gq_store)

        pairs = []
        for h in range(8):
            for qb in range(NB):
                nk = 4 * qb + 4
                for kt in range(nk):
                    pairs.append((h, qb, kt, kt == 0, kt == nk - 1))
        LA = 2
        pending = []
        gidx = [0]

        def emit_front(n):
            h, qb, kt, first, last = pairs[n]
            if first and qb == 0:
                if h == 0:
                    load_head(0)
                if h + 1 < 8:
                    load_head(h + 1)
            g = h * NB + qb
            q_, k_ = QTh[h % 2], KTh[h % 2]
            pt = ps[n % 4]
            di = kt - 4 * qb
            c0 = 128 * di if di > 0 else 0
            qs = slice(c0, 512)
            P.op("pe", lambda e: e.matmul(pt.ap[:, qs], lhsT=k_.ap[:, kt * 128:(kt + 1) * 128],
                                          rhs=q_.ap[:, qb * 512 + c0:(qb + 1) * 512], start=True, stop=True),
                 reads=[k_, q_], writes=[pt])
            p_ = pT[n % 4]
            if di >= 0:
                dt_ = dtmp[n % 2]
                P.op("dve", lambda e: e.tensor_tensor(out=dt_.ap[:, qs], in0=pt.ap[:, qs], in1=mk.ap[:, di, qs], op=ALU.add),
                     reads=[pt, mk], writes=[dt_])
                P.op("act", lambda e: e.activation(out=p_.ap[:, qs], in_=dt_.ap[:, qs], func=AF.Exp, scale=0.125),
                     reads=[dt_], writes=[p_])
            else:
                P.op("act", lambda e: e.activation(out=p_.ap, in_=pt.ap, func=AF.Exp, scale=0.125),
                     reads=[pt], writes=[p_])

        def emit_back(n):
            h, qb, kt, first, last = pairs[n]
            g = h * NB + qb
            po = ps[4 + g % 2]
            p_ = pT[n % 4]
            di = kt - 4 * qb
            qs = slice(128 * di if di > 0 else 0, 512)
            P.op("pe", lambda e: e.matmul(po.ap[0:65, qs], lhsT=Vaug.ap[:, kt, h, :], rhs=p_.ap[:, qs], start=first, stop=last),
                 reads=[Vch[kt // VC], p_], writes=[po])
            if last:
                o_, rz_, y_ = osb[g % 2], rzb[g % 2], yst[g % 2]
                P.op("dve", lambda e: e.tensor_copy(out=o_.ap[0:65, :], in_=po.ap[0:65, :]), reads=[po], writes=[o_])

                def later():
                    P.op("pe", lambda e: e.matmul(ps[6].ap[0:64, :], lhsT=sel.ap[0:65, :], rhs=o_.ap[0:65, :], start=True, stop=True),
                         reads=[sel, o_], writes=[ps[6]])
                    P.op("dve", lambda e: e.reciprocal(out=rz_.ap[0:64, :], in_=ps[6].ap[0:64, :]), reads=[ps[6]], writes=[rz_])
                    P.op("dve", lambda e: e.tensor_tensor(out=y_.ap[0:64, :], in0=o_.ap[0:64, :], in1=rz_.ap[0:64, :], op=ALU.mult),
                         reads=[o_, rz_], writes=[y_])
                    dst = yT[1024 + h * 64:1024 + (h + 1) * 64, qb * 512:(qb + 1) * 512]
                    P.dma("sp", lambda e, s: e.dma_start(out=dst, in_=y_.ap[0:64, :]).then_inc(s, 16), 1, "y%d" % (g % 2), reads=[y_])
                pending.append([2, later])

        for n in range(len(pairs) + LA):
            if n < len(pairs):
                emit_front(n)
            if n - LA >= 0:
                emit_back(n - LA)
            for it in list(pending):
                it[0] -= 1
                if it[0] <= 0:
                    it[1]()
                    pending.remove(it)
        for it in pending:
            it[1]()

    def phase_B(l):
        lam_init = 0.8 - 0.6 * math.exp(-0.3 * l)
        lamt = sb("B_lam", [256], F32)
        lprod = sb("B_lprod", [128], F32)
        lsum = sb("B_lsum", [2], F32)
        neglam = sb("B_neglam", [1], F32)
        sgc = sb("B_sgc", [1], F32)
        epsb = sb("B_eps", [1], F32)

        def fl_(e, s):
            with nc.allow_non_contiguous_dma(reason="small"):
                e.dma_start(out=lamt.ap.unsqueeze(1), in_=lambdas[l].partition_broadcast(128)).then_inc(s, 16)
                e.dma_start(out=sgc.ap, in_=sub_gain[l].unsqueeze(1)).then_inc(s, 16)
        P.dma("sp", fl_, 2, "g", writes=[lamt, sgc])
        P.op("dve", lambda e: e.tensor_tensor(out=lprod.ap[:, 0:64], in0=lamt.ap[:, 0:64], in1=lamt.ap[:, 64:128], op=ALU.mult),
             reads=[lamt], writes=[lprod])
        P.op("dve", lambda e: e.tensor_tensor(out=lprod.ap[:, 64:128], in0=lamt.ap[:, 128:192], in1=lamt.ap[:, 192:256], op=ALU.mult),
             reads=[lamt, lprod], writes=[lprod])
        P.op("dve", lambda e: e.reduce_sum(out=lsum.ap, in_=lprod.ap.rearrange("p (a b) -> p a b", a=2), axis=mybir.AxisListType.X),
             reads=[lprod], writes=[lsum])
        P.op("act", lambda e: e.activation(out=lsum.ap, in_=lsum.ap, func=AF.Exp), reads=[lsum], writes=[lsum])
        P.op("dve", lambda e: e.tensor_tensor(out=neglam.ap, in0=lsum.ap[:, 1:2], in1=lsum.ap[:, 0:1], op=ALU.subtract),
             reads=[lsum], writes=[neglam])
        P.op("dve", lambda e: e.tensor_scalar(out=neglam.ap, in0=neglam.ap, scalar1=-lam_init, scalar2=None, op0=ALU.add),
             reads=[neglam], writes=[neglam])
        P.op("dve", lambda e: e.tensor_scalar(out=sgc.ap, in0=sgc.ap, scalar1=1.0 - lam_init, scalar2=None, op0=ALU.mult),
             reads=[sgc], writes=[sgc])
        P.op("dve", lambda e: e.memset(epsb.ap, EPS), writes=[epsb])

        mk = sb("B_mask", [4, 512], F32)
        P.dma("sp", lambda e, s: e.dma_start(out=mk.ap, in_=c_maskB.rearrange("i k q -> k i q")).then_inc(s, 16), 1, "g", writes=[mk])
        Qh = [sb("B_Q%d" % i, [S], BF16) for i in range(2)]
        K0 = [sb("B_K0%d" % i, [S], BF16) for i in range(2)]
        K1 = [sb("B_K1%d" % i, [S], BF16) for i in range(2)]
        Vh = [sb("B_V%d" % i, [NT, 128], BF16) for i in range(2)]
        for i in range(2):
            P.op("pool", lambda e, i=i: e.memset(K0[i].ap[64:128, :], 0.0), writes=[K0[i]])
            P.op("pool", lambda e, i=i: e.memset(K1[i].ap[0:64, :], 0.0), writes=[K1[i]])
        pT = [sb("B_p%d" % i, [512], BF16) for i in range(6)]
        zacc = [[sb("B_z%d%d" % (i, c), [512], F32) for c in range(2)] for i in range(2)]
        sbanks = [ps[0], ps[1], ps[2], ps[6]]
        dtmp = [sb("B_dt%d" % i, [512], F32) for i in range(2)]
        o0b = [sb("B_o0%d" % i, [512], F32) for i in range(2)]
        o1b = [sb("B_o1%d" % i, [512], F32) for i in range(2)]
        r0b = [sb("B_r0%d" % i, [512], F32) for i in range(2)]
        r1b = [sb("B_r1%d" % i, [512], F32) for i in range(2)]
        sqb = [sb("B_sq%d" % i, [512], F32) for i in range(2)]
        yst = [sb("B_y%d" % i, [512], BF16) for i in range(2)]

        def load_head(h):
            q_, k0_, k1_, v_ = Qh[h % 2], K0[h % 2], K1[h % 2], Vh[h % 2]
            P.dma("sp", lambda e, s: e.dma_start(out=q_.ap, in_=QT["b"][h * 128:(h + 1) * 128, :]).then_inc(s, 16),
                  1, "q%d" % (h % 2), writes=[q_])
            P.dma("sp", lambda e, s: e.dma_start(out=k0_.ap[0:64, :], in_=KT["b"][h * 128:h * 128 + 64, :]).then_inc(s, 16),
                  1, "k%d" % (h % 2), reads=[k0_], writes=[k0_])
            P.dma("sp", lambda e, s: e.dma_start(out=k1_.ap[64:128, :], in_=KT["b"][h * 128 + 64:(h + 1) * 128, :]).then_inc(s, 16),
                  1, "kk%d" % (h % 2), reads=[k1_], writes=[k1_])
            srcv = V["b"][:, h * 128:(h + 1) * 128].rearrange("(t p) c -> p t c", p=128)
            TC = min(8, NT)

            def fvl(e, s):
                for t0 in range(0, NT, TC):
                    e.dma_start(out=v_.ap[:, t0:t0 + TC, :], in_=srcv[:, t0:t0 + TC, :]).then_inc(s, 16)
            P.dma("sp", fvl, NT // TC, "cS%d" % (h % 2), writes=[v_])

        pairs = []
        for h in range(4):
            for qb in range(NB):
                nk = 4 * qb + 4
                for kt in range(nk):
                    pairs.append((h, qb, kt, kt == 0, kt == nk - 1))
        LA = 1
        pending = []
        sidx = [0]
        smap = {}

        def emit_front(n):
            h, qb, kt, first, last = pairs[n]
            q_ = Qh[h % 2]
            di = kt - 4 * qb
            for c, k_ in ((0, K0[h % 2]), (1, K1[h % 2])):
                def one(c, k_):
                    ti = sidx[0]
                    sidx[0] += 1
                    pt = sbanks[ti % 4]
                    p_ = pT[ti % 6]
                    smap[(n, c)] = p_
                    c0 = 128 * di if di > 0 else 0
                    qs = slice(c0, 512)
                    P.op("pe", lambda e: e.matmul(pt.ap[:, qs], lhsT=k_.ap[:, kt * 128:(kt + 1) * 128],
                                                  rhs=q_.ap[:, qb * 512 + c0:(qb + 1) * 512],
                                                  start=True, stop=True), reads=[k_, q_], writes=[pt])
                    if di >= 0:
                        dt_ = dtmp[ti % 2]
                        P.op("dve", lambda e: e.tensor_tensor(out=dt_.ap[:, qs], in0=pt.ap[:, qs], in1=mk.ap[:, di, qs], op=ALU.add),
                             reads=[pt, mk], writes=[dt_])
                        P.op("act", lambda e: e.activation(out=p_.ap[:, qs], in_=dt_.ap[:, qs], func=AF.Exp, scale=0.125),
                             reads=[dt_], writes=[p_])
                    else:
                        P.op("act", lambda e: e.activation(out=p_.ap, in_=pt.ap, func=AF.Exp, scale=0.125),
                             reads=[pt], writes=[p_])
                one(c, k_)

        def emit_back(n):
            h, qb, kt, first, last = pairs[n]
            g = h * NB + qb
            v_ = Vh[h % 2]
            di = kt - 4 * qb
            qs = slice(128 * di if di > 0 else 0, 512)
            for c in (0, 1):
                def one(c):
                    p_ = smap.pop((n, c))
                    po = ps[4 + c]
                    za = zacc[g % 2][c]
                    P.op("pe", lambda e: e.matmul(po.ap[:, qs], lhsT=v_.ap[:, kt, :], rhs=p_.ap[:, qs], start=first, stop=last),
                         reads=[v_, p_], writes=[po])
                    if first:
                        P.op("dve", lambda e: e.tensor_copy(out=za.ap, in_=p_.ap), reads=[p_], writes=[za])
                    else:
                        P.op("dve", lambda e: e.tensor_tensor(out=za.ap[:, qs], in0=za.ap[:, qs], in1=p_.ap[:, qs], op=ALU.add),
                             reads=[za, p_], writes=[za])
                one(c)
            if last:
                o0, o1, r0, r1, sq_, y_ = o0b[g % 2], o1b[g % 2], r0b[g % 2], r1b[g % 2], sqb[g % 2], yst[g % 2]
                za0, za1 = zacc[g % 2]
                P.op("act", lambda e: e.activation(out=o0.ap, in_=ps[4].ap, func=AF.Copy), reads=[ps[4]], writes=[o0])
                P.op("act", lambda e: e.activation(out=o1.ap, in_=ps[5].ap, func=AF.Copy), reads=[ps[5]], writes=[o1])

                def later2():
                    P.op("pe", lambda e: e.matmul(ps[7].ap, lhsT=ones_f.ap, rhs=sq_.ap, start=True, stop=True),
                         reads=[ones_f, sq_], writes=[ps[7]])
                    P.op("act", lambda e: e.activation(out=r0.ap, in_=ps[7].ap, func=AF.Ln, scale=1.0 / 128, bias=epsb.ap),
                         reads=[ps[7], epsb], writes=[r0])
                    P.op("act", lambda e: e.activation(out=r1.ap, in_=r0.ap, func=AF.Exp, scale=-0.5), reads=[r0], writes=[r1])
                    P.op("dve", lambda e: e.scalar_tensor_tensor(out=y_.ap, in0=o0.ap, scalar=sgc.ap[:, 0:1], in1=r1.ap,
                                                                 op0=ALU.mult, op1=ALU.mult), reads=[o0, sgc, r1], writes=[y_])
                    dst = yT[512 + h * 128:512 + (h + 1) * 128, qb * 512:(qb + 1) * 512]
                    P.dma("sp", lambda e, s: e.dma_start(out=dst, in_=y_.ap).then_inc(s, 16), 1, "y%d" % (g % 2), reads=[y_])

                def later1():
                    P.op("pe", lambda e: e.matmul(ps[7].ap, lhsT=ones_f.ap, rhs=za0.ap, start=True, stop=True),
                         reads=[ones_f, za0], writes=[ps[7]])
                    P.op("pe", lambda e: e.matmul(ps[3].ap, lhsT=ones_f.ap, rhs=za1.ap, start=True, stop=True),
                         reads=[ones_f, za1], writes=[ps[3]])
                    P.op("act", lambda e: e.activation(out=r0.ap, in_=ps[7].ap, func=AF.Ln), reads=[ps[7]], writes=[r0])
                    P.op("act", lambda e: e.activation(out=r0.ap, in_=r0.ap, func=AF.Exp, scale=-1.0), reads=[r0], writes=[r0])
                    P.op("act", lambda e: e.activation(out=r1.ap, in_=ps[3].ap, func=AF.Ln), reads=[ps[3]], writes=[r1])
                    P.op("act", lambda e: e.activation(out=r1.ap, in_=r1.ap, func=AF.Exp, scale=-1.0), reads=[r1], writes=[r1])
                    P.op("dve", lambda e: e.tensor_tensor(out=o0.ap, in0=o0.ap, in1=r0.ap, op=ALU.mult), reads=[o0, r0], writes=[o0])
                    P.op("dve", lambda e: e.tensor_tensor(out=o1.ap, in0=o1.ap, in1=r1.ap, op=ALU.mult), reads=[o1, r1], writes=[o1])
                    P.op("dve", lambda e: e.scalar_tensor_tensor(out=o0.ap, in0=o1.ap, scalar=neglam.ap[:, 0:1], in1=o0.ap,
                                                                 op0=ALU.mult, op1=ALU.add), reads=[o1, o0, neglam], writes=[o0])
                    P.op("pool", lambda e: e.tensor_tensor(out=sq_.ap, in0=o0.ap, in1=o0.ap, op=ALU.mult), reads=[o0], writes=[sq_])
                    pending.append([3, later2])
                pending.append([2, later1])

        load_head(0)
        for n in range(len(pairs) + LA):
            if n < len(pairs):
                emit_front(n)
            if n - LA >= 0:
                emit_back(n - LA)
                hh, qq, kk, ff, ll = pairs[n - LA]
                if ff and qq == 0 and hh + 1 < 4:
                    load_head(hh + 1)
            for it in list(pending):
                it[0] -= 1
                if it[0] <= 0:
                    it[1]()
                    pending.remove(it)
        while pending:
            it = pending.pop(0)
            it[1]()

    def phase_M(l):
        tabR = sb("M_tab", [5, 8, 128], F32)

        def ft(e, s):
            for o in range(5):
                e.dma_start(out=tabR.ap[:, o, :, :], in_=biasA[l, 4 - o]).then_inc(s, 16)
        P.dma("sp", ft, 5, "g", writes=[tabR])
        tab8 = sb("M_tab8", [5, 8, 128], BF16)
        idb = sb("M_idb", [128], BF16)
        P.op("dve", lambda e: e.tensor_scalar(out=tab8.ap, in0=tabR.ap, scalar1=8.0, scalar2=None, op0=ALU.mult),
             reads=[tabR], writes=[tab8])
        P.op("dve", lambda e: e.tensor_copy(out=idb.ap, in_=ident.ap), reads=[ident], writes=[idb])
        sel = sb("M_sel", [64], F32)
        P.op("dve", lambda e: e.memset(sel.ap, 0.0), writes=[sel])
        P.op("dve", lambda e: e.memset(sel.ap[64:65, :], 1.0), reads=[sel], writes=[sel])
        Qh = [sb("M_Q%d" % i, [S], BF16) for i in range(2)]
        K0 = [sb("M_K0%d" % i, [S], BF16) for i in range(2)]
        K1 = [sb("M_K1%d" % i, [S], BF16) for i in range(2)]
        Vh = [sb("M_V%d" % i, [NT, 2, 65], BF16) for i in range(2)]
        vst1 = sb("M_vst", [NT, 128], BF16)
        vst = [vst1, vst1]
        for i in range(2):
            P.op("pool", lambda e, i=i: e.memset(K0[i].ap[64:128, :], 0.0), writes=[K0[i]])
            P.op("pool", lambda e, i=i: e.memset(K1[i].ap[0:64, :], 0.0), writes=[K1[i]])
            P.op("pool", lambda e, i=i: e.memset(Vh[i].ap[:, :, :, 64:65], 1.0), writes=[Vh[i]])
        pT = [sb("M_p%d" % i, [512], BF16) for i in range(4)]
        osb = [sb("M_o%d" % i, [512], F32) for i in range(2)]
        rzb = [sb("M_rz%d" % i, [512], F32) for i in range(2)]
        yst = [sb("M_y%d" % i, [512], BF16) for i in range(2)]

        def load_pair(hp):
            q_, k0_, k1_, v_, vs_ = Qh[hp % 2], K0[hp % 2], K1[hp % 2], Vh[hp % 2], vst[hp % 2]
            P.dma("sp", lambda e, s: e.dma_start(out=q_.ap, in_=QT["a"][hp * 128:(hp + 1) * 128, :]).then_inc(s, 16),
                  1, "q%d" % (hp % 2), writes=[q_])
            P.dma("sp", lambda e, s: e.dma_start(out=k0_.ap[0:64, :], in_=KT["a"][hp * 128:hp * 128 + 64, :]).then_inc(s, 16),
                  1, "k%d" % (hp % 2), reads=[k0_], writes=[k0_])
            P.dma("sp", lambda e, s: e.dma_start(out=k1_.ap[64:128, :], in_=KT["a"][hp * 128 + 64:(hp + 1) * 128, :]).then_inc(s, 16),
                  1, "kk%d" % (hp % 2), reads=[k1_], writes=[k1_])
            srcv = V["a"][:, hp * 128:(hp + 1) * 128].rearrange("(t p) c -> p t c", p=128)
            TC = min(8, NT)

            def fvl(e, s):
                for t0 in range(0, NT, TC):
                    e.dma_start(out=vs_.ap[:, t0:t0 + TC, :], in_=srcv[:, t0:t0 + TC, :]).then_inc(s, 16)
            P.dma("sp", fvl, NT // TC, "cS%d" % (hp % 2), writes=[vs_])
            P.op("pool", lambda e: e.tensor_copy(out=v_.ap[:, :, :, 0:64], in_=vs_.ap.rearrange("p t (h d) -> p t h d", h=2)),
                 reads=[vs_, v_], writes=[v_])

        pairs = []
        for hp in range(4):
            for hd in range(2):
                for qb in range(NB):
                    order = [3, 0, 1, 2, 4, 5, 6, 7] if qb >= 1 else [4, 5, 6, 7]
                    for ii, i in enumerate(order):
                        pairs.append((hp, hd, qb, i, ii == 0, ii == len(order) - 1))
        LA = 2
        pending = []

        def geom(qb, i):
            a_, b_ = max(0, i - 4), min(3, i)
            return a_, b_, 4 * qb - 4 + i

        def emit_front(n):
            hp, hd, qb, i, first, last = pairs[n]
            a_, b_, kt = geom(qb, i)
            h = 2 * hp + hd
            q_ = Qh[hp % 2]
            k_ = (K0 if hd == 0 else K1)[hp % 2]
            pt = ps[n % 4]
            p_ = pT[n % 4]
            qs = slice(128 * a_, 128 * (b_ + 1))
            def fqk(e):
                e.matmul(pt.ap[:, qs], lhsT=k_.ap[:, kt * 128:(kt + 1) * 128],
                         rhs=q_.ap[:, qb * 512 + 128 * a_:qb * 512 + 128 * (b_ + 1)], start=True, stop=False)
                for t_ in range(a_, b_ + 1):
                    ins = e.matmul(pt.ap[:, 128 * t_:128 * (t_ + 1)], lhsT=idb.ap, rhs=tab8.ap[:, t_ + 4 - i, h, :],
                                   start=False, stop=(t_ == b_))
                return ins
            P.op("pe", fqk, reads=[k_, q_, idb, tab8], writes=[pt])
            P.op("act", lambda e: e.activation(out=p_.ap[:, qs], in_=pt.ap[:, qs], func=AF.Exp, scale=0.125), reads=[pt], writes=[p_])

        def emit_back(n):
            hp, hd, qb, i, first, last = pairs[n]
            a_, b_, kt = geom(qb, i)
            h = 2 * hp + hd
            g = h * NB + qb
            po = ps[4 + g % 2]
            p_ = pT[n % 4]
            v_ = Vh[hp % 2]
            qs = slice(128 * a_, 128 * (b_ + 1))
            P.op("pe", lambda e: e.matmul(po.ap[0:65, qs], lhsT=v_.ap[:, kt, hd, :], rhs=p_.ap[:, qs], start=first, stop=last),
                 reads=[v_, p_], writes=[po])
            if last:
                o_, rz_, y_ = osb[g % 2], rzb[g % 2], yst[g % 2]
                P.op("dve", lambda e: e.tensor_copy(out=o_.ap[0:65, :], in_=po.ap[0:65, :]), reads=[po], writes=[o_])

                def later():
                    P.op("pe", lambda e: e.matmul(ps[6].ap[0:64, :], lhsT=sel.ap[0:65, :], rhs=o_.ap[0:65, :], start=True, stop=True),
                         reads=[sel, o_], writes=[ps[6]])
                    P.op("dve", lambda e: e.reciprocal(out=rz_.ap[0:64, :], in_=ps[6].ap[0:64, :]), reads=[ps[6]], writes=[rz_])
                    P.op("dve", lambda e: e.tensor_tensor(out=y_.ap[0:64, :], in0=o_.ap[0:64, :], in1=rz_.ap[0:64, :], op=ALU.mult),
                         reads=[o_, rz_], writes=[y_])
                    dst = yT[h * 64:(h + 1) * 64, qb * 512:(qb + 1) * 512]
                    P.dma("sp", lambda e, s: e.dma_start(out=dst, in_=y_.ap[0:64, :]).then_inc(s, 16), 1, "y%d" % (g % 2), reads=[y_])
                pending.append([2, later])

        load_pair(0)
        for n in range(len(pairs) + LA):
            if n < len(pairs):
                emit_front(n)
            if n - LA >= 0:
                emit_back(n - LA)
                hp_, hd_, qq, i_, ff, ll = pairs[n - LA]
                if ff and qq == 0 and hd_ == 0 and hp_ + 1 < 4:
                    load_pair(hp_ + 1)
            for it in list(pending):
                it[0] -= 1
                if it[0] <= 0:
                    it[1]()
                    pending.remove(it)
        for it in pending:
            it[1]()

    def load_w_cast(dst_tile, src3, ncols, key):
        nch = src3.shape[1]
        pieces = [(a_, min(a_ + 2048, ncols)) for a_ in range(0, ncols, 2048)]

        def f(e, s):
            for c in range(nch):
                for a_, b_ in pieces:
                    e.dma_start(out=dst_tile.ap[:, c, a_:b_], in_=src3[:, c, a_:b_]).then_inc(s, 16)
        P.dma("pool", f, nch * len(pieces), key, writes=[dst_tile])

    def col_load(dst_tile, src1, ncol, key):
        def f(e, s):
            with nc.allow_non_contiguous_dma(reason="small"):
                e.dma_start(out=dst_tile.ap, in_=src1.rearrange("(c p) -> p c", p=128)).then_inc(s, 16)
        P.dma("sp", f, 1, key, writes=[dst_tile])

    def phase_D1(l):
        wg = sb("D_wg", [NCH, 3 * D], BF16)
        wb = [sb("D_wb%d" % i, [4, D], BF16) for i in range(3)]
        wo = sb("D_wo", [NCH, D], BF16)
        bg = sb("D_bg", [24], F32)
        gcol = sb("D_g", [NCH], F32)
        xblk = [sb("D_x%d" % i, [NCH, 512], F32) for i in range(2)]
        sq = sb("D_sq", [NCH, 512], BF16)
        hT = [sb("D_h%d" % i, [NCH, 512], BF16) for i in range(2)]
        rstd = sb("D_rstd", [512], F32)
        tmp = sb("D_tmp", [512], F32)
        yblk = [sb("D_y%d" % i, [12, 512], BF16) for i in range(2)]
        mg = sb("D_mg", [NCH, 512], BF16)
        gb = [sb("D_gb%d" % i, [512], F32) for i in range(3)]
        macc = [sb("D_m%d" % i, [512], F32) for i in range(2)]
        tt_ = [sb("D_t%d" % i, [512], F32) for i in range(2)]
        load_w_cast(wg, w_gate[l].rearrange("(c p) n -> p c n", p=128), 3 * D, "w0")
        for i in range(3):
            load_w_cast(wb[i], w_br[i][l].rearrange("(c p) n -> p c n", p=128), D, "w1")
        load_w_cast(wo, w_out[l].rearrange("(c p) n -> p c n", p=128), D, "w0")
        col_load(bg, b_gate[l], 24, "g")
        col_load(gcol, norm_mix[l], NCH, "bf")

        bank = [1]

        def next_bank():
            b = bank[0]
            bank[0] = b + 1 if b < 7 else 1
            return ps[b]
        cnt = [0]

        def mm_group(pt, n, lhs_fn, rhs_fn, reads):
            def f(e):
                for c in range(n):
                    ins = e.matmul(pt.ap, lhsT=lhs_fn(c), rhs=rhs_fn(c), start=(c == 0), stop=(c == n - 1))
                return ins
            P.op("pe", f, reads=reads, writes=[pt])

        def load_block(j):
            xb_, yb_ = xblk[j % 2], yblk[j % 2]
            src = xT[:, :, j * 512:(j + 1) * 512].rearrange("c p t -> p c t")
            P.dma("sp", lambda e, s: e.dma_start(out=xb_.ap, in_=src).then_inc(s, 16), 1, "x%d" % (j % 2), writes=[xb_])
            srcy = yT[:, j * 512:(j + 1) * 512].rearrange("(c p) t -> p c t", p=128)
            P.dma("sp", lambda e, s: e.dma_start(out=yb_.ap, in_=srcy).then_inc(s, 16), 1, "cC%d" % (j % 2), writes=[yb_])

        def norm_block(j):
            emit_norm_block(j, xblk[j % 2], sq, hT[j % 2], gcol, rstd, tmp, ps[0])

        def merge_chunk(j, dch):
            h_, yb_ = hT[j % 2], yblk[j % 2]
            m_ = macc[dch % 2]
            for br in range(3):
                def one(br):
                    pg = next_bank()
                    col = br * D + dch * 128
                    mm_group(pg, NCH, lambda c: wg.ap[:, c, col:col + 128], lambda c: h_.ap[:, c, :], [wg, h_])
                    g_ = gb[cnt[0] % 3]
                    t_ = tt_[cnt[0] % 2]
                    cnt[0] += 1
                    P.op("act", lambda e: e.activation(out=g_.ap, in_=pg.ap, func=AF.Sigmoid, bias=bg.ap[:, br * 8 + dch:br * 8 + dch + 1]),
                         reads=[pg, bg], writes=[g_])
                    pb = next_bank()
                    mm_group(pb, 4, lambda c: wb[br].ap[:, c, dch * 128:(dch + 1) * 128], lambda c: yb_.ap[:, br * 4 + c, :], [wb[br], yb_])
                    if br == 0:
                        P.op("dve", lambda e: e.tensor_tensor(out=m_.ap, in0=pb.ap, in1=g_.ap, op=ALU.mult), reads=[pb, g_], writes=[m_])
                    else:
                        P.op("dve", lambda e: e.tensor_tensor(out=t_.ap, in0=pb.ap, in1=g_.ap, op=ALU.mult), reads=[pb, g_], writes=[t_])
                        if br == 1:
                            P.op("pool", lambda e: e.tensor_tensor(out=m_.ap, in0=m_.ap, in1=t_.ap, op=ALU.add), reads=[m_, t_], writes=[m_])
                        else:
                            P.op("pool", lambda e: e.tensor_tensor(out=mg.ap[:, dch, :], in0=m_.ap, in1=t_.ap, op=ALU.add),
                                 reads=[m_, t_], writes=[mg])
                one(br)

        def out_chunk(j, dch):
            xb_ = xblk[j % 2]
            po = next_bank()
            mm_group(po, NCH, lambda c: wo.ap[:, c, dch * 128:(dch + 1) * 128], lambda c: mg.ap[:, c, :], [wo, mg])
            P.op("dve", lambda e: e.tensor_tensor(out=xb_.ap[:, dch, :], in0=po.ap, in1=xb_.ap[:, dch, :], op=ALU.add),
                 reads=[po, xb_], writes=[xb_])

        load_block(0)
        norm_block(0)
        for j in range(NB):
            if j + 1 < NB:
                load_block(j + 1)
            for dch in range(NCH):
                merge_chunk(j, dch)
                if dch == 3 and j + 1 < NB:
                    norm_block(j + 1)
            for dch in range(NCH):
                out_chunk(j, dch)
            xb_ = xblk[j % 2]
            dst = xT[:, :, j * 512:(j + 1) * 512].rearrange("c p t -> p c t")

            def st(xb_, dst, j):
                P.dma("sp", lambda e, s: e.dma_start(out=dst, in_=xb_.ap).then_inc(s, 16), 1, "y%d" % (j % 2), reads=[xb_])
            st(xb_, dst, j)

    def phase_D2(l):
        w1 = sb("E_w1", [NCH, DFF], BF16)
        w2 = sb("E_w2", [DFF // 128, D], BF16)
        gcol = sb("E_g", [NCH], F32)
        xblk = sb("E_x", [NCH, 512], F32)
        sq = sb("E_sq", [NCH, 512], BF16)
        hT = sb("E_h", [NCH, 512], BF16)
        uT = sb("E_u", [DFF // 128, 512], BF16)
        rl = [sb("E_rl%d" % i, [512], F32) for i in range(2)]
        rstd, tmp = rl[0], rl[1]
        xr = [sb("E_xr%d" % i, [512], F32) for i in range(2)]
        load_w_cast(w1, w_ff1[l].rearrange("(c p) n -> p c n", p=128), DFF, "w0")
        load_w_cast(w2, w_ff2[l].rearrange("(c p) n -> p c n", p=128), D, "w1")
        col_load(gcol, norm_mlp[l], NCH, "g")
        bank = [1]

        def next_bank():
            b = bank[0]
            bank[0] = b + 1 if b < 7 else 1
            return ps[b]

        def mm_group(pt, n, lhs_fn, rhs_fn, reads):
            def f(e):
                for c in range(n):
                    ins = e.matmul(pt.ap, lhsT=lhs_fn(c), rhs=rhs_fn(c), start=(c == 0), stop=(c == n - 1))
                return ins
            P.op("pe", f, reads=reads, writes=[pt])

        def load_block(j):
            src = xT[:, :, j * 512:(j + 1) * 512].rearrange("c p t -> p c t")
            P.dma("sp", lambda e, s: e.dma_start(out=xblk.ap, in_=src).then_inc(s, 16), 1, "x0", writes=[xblk])

        def ff1_chunk(j, fc):
            pt = next_bank()
            mm_group(pt, NCH, lambda c: w1.ap[:, c, fc * 128:(fc + 1) * 128], lambda c: hT.ap[:, c, :], [w1, hT])
            r_ = rl[fc % 2]
            P.op("act", lambda e: e.activation(out=r_.ap, in_=pt.ap, func=AF.Relu), reads=[pt], writes=[r_])
            P.op("pool", lambda e: e.tensor_tensor(out=uT.ap[:, fc, :], in0=r_.ap, in1=r_.ap, op=ALU.mult), reads=[r_], writes=[uT])

        def ff2_chunk(j, dch):
            pt = next_bank()
            mm_group(pt, DFF // 128, lambda c: w2.ap[:, c, dch * 128:(dch + 1) * 128], lambda c: uT.ap[:, c, :], [w2, uT])
            x_ = xr[dch % 2]
            o_ = x_
            P.dma("sp", lambda e, s: e.dma_start(out=x_.ap, in_=xT[dch, :, j * 512:(j + 1) * 512]).then_inc(s, 16), 1,
                  "cS%d" % (dch % 2), writes=[x_])
            P.op("dve", lambda e: e.tensor_tensor(out=o_.ap, in0=pt.ap, in1=x_.ap, op=ALU.add), reads=[pt, x_], writes=[o_])
            P.dma("sp", lambda e, s: e.dma_start(out=xT[dch, :, j * 512:(j + 1) * 512], in_=o_.ap).then_inc(s, 16), 1,
                  "y%d" % (dch % 2), reads=[o_])

        load_block(0)
        emit_norm_block(0, xblk, sq, hT, gcol, rstd, tmp, ps[0])
        for j in range(NB):
            if j + 1 < NB:
                load_block(j + 1)
            for fc in range(DFF // 128):
                ff1_chunk(j, fc)
            for dch in range(NCH):
                ff2_chunk(j, dch)
                if dch == 0 and j + 1 < NB:
                    emit_norm_block(j + 1, xblk, sq, hT, gcol, rstd, tmp, ps[0])

    def phase_F():
        gcol = sb("F_g", [NCH], F32)
        xblk = [sb("F_x%d" % i, [NCH, 512], F32) for i in range(2)]
        sq = sb("F_sq", [NCH, 512], BF16)
        hF = sb("F_h", [NCH, 512], F32)
        rstd = sb("F_rstd", [512], F32)
        tmp = sb("F_tmp", [512], F32)
        ot = [sb("F_o%d" % i, [D], F32) for i in range(2)]
        col_load(gcol, final_norm, NCH, "g")
        bank = [1]

        def next_bank():
            b = bank[0]
            bank[0] = b + 1 if b < 7 else 1
            return ps[b]

        def load_block(j):
            xb_ = xblk[j % 2]
            src = xT[:, :, j * 512:(j + 1) * 512].rearrange("c p t -> p c t")
            P.dma("sp", lambda e, s: e.dma_start(out=xb_.ap, in_=src).then_inc(s, 16), 1, "x%d" % (j % 2), writes=[xb_])
        load_block(0)
        k = [0]
        for j in range(NB):
            if j + 1 < NB:
                load_block(j + 1)
            emit_norm_block(j, xblk[j % 2], sq, hF, gcol, rstd, tmp, ps[0])
            for tt in range(4):
                def one(j, tt):
                    o_ = ot[k[0] % 2]
                    k[0] += 1
                    for half in range(2):
                        def hh(half):
                            pt = next_bank()

                            def f(e):
                                for cc in range(4):
                                    c = half * 4 + cc
                                    ins = e.transpose(out=pt.ap[:, cc * 128:(cc + 1) * 128], in_=hF.ap[:, c, tt * 128:(tt + 1) * 128],
                                                      identity=ident.ap)
                                return ins
                            P.op("pe", f, reads=[hF, ident], writes=[pt])
                            if half == 0:
                                P.op("act", lambda e: e.activation(out=o_.ap[:, 0:512], in_=pt.ap, func=AF.Copy), reads=[pt], writes=[o_])
                            else:
                                P.op("dve", lambda e: e.tensor_copy(out=o_.ap[:, 512:1024], in_=pt.ap), reads=[pt], writes=[o_])
                        hh(half)
                    dst = out[j * 512 + tt * 128:j * 512 + (tt + 1) * 128, :]
                    P.dma("sp", lambda e, s: e.dma_start(out=dst, in_=o_.ap).then_inc(s, 16), 1, "y%d" % ((k[0] - 1) % 2), reads=[o_])
                one(j, tt)

    only = stop_after if isinstance(stop_after, (list, tuple)) else None
    if only is not None:
        stop_after = None

    def want(ph):
        return only is None or ph in only
    phase_pre()
    P.barrier()
    for l in range(n_layers):
        done = False
        for nm, fn in (("A", phase_A), ("C", phase_C), ("B", phase_B), ("M", phase_M), ("D1", phase_D1), ("D2", phase_D2)):
            if want(nm):
                arena.reset(persist_mark)
                fn(l)
                P.barrier()
            if stop_after == nm:
                done = True
                break
        if done:
            break
    if stop_after is None and want("F"):
        arena.reset(persist_mark)
        phase_F()

    P.barrier()
    P.emit()
    return nc


def host_constants(S):
    NB = S // 512
    ident = np.eye(128, dtype=np.float32)
    BT = np.zeros((128, 128), np.float32)
    for k in range(8 * NB):
        for m in range(8 * NB):
            if k // NB == m // NB and k % NB < m % NB:
                BT[k, m] = 1.0
    inv_freq = np.power(np.float32(500000.0), -np.arange(0, 16, 2, dtype=np.float32) / np.float32(16)).astype(np.float32)
    rope = np.zeros((128, 2), np.float32)
    for r in range(128):
        rr = r % 64
        if rr < 16:
            rope[r, 0] = inv_freq[rr % 8]
            rope[r, 1] = -1.0 if rr < 8 else 1.0
        else:
            rope[r, 0] = 0.0
            rope[r, 1] = 1.0
    maskC = np.zeros((4, 128, 512), np.float32)
    maskB = np.zeros((4, 128, 512), np.float32)
    k = np.arange(128)[:, None]
    q = np.arange(512)[None, :]
    for i in range(4):
        kk = 128 * i + k
        maskC[i] = np.where(kk <= q, 0.0, NEG)
        maskB[i] = np.where((kk // 64) <= (q // 64), 0.0, NEG)
    return dict(c_ident=ident, c_rope=rope, c_maskC=maskC, c_maskB=maskB, c_BT=BT)


def host_biasA(rel_bias):
    L = rel_bias.shape[0]
    k = np.arange(128)[:, None]
    q = np.arange(128)[None, :]
    tab = np.empty((L, 5, 128, 8, 128), np.float32)
    for i in range(5):
        koff = 128 * (i - 4) + k
        rel = np.clip(koff - q, -128, 128) + 128
        qc = q // 64
        valid = (koff >= 64 * qc - 512) & (koff < 64 * qc + 64)
        g = rel_bias[:, :, rel]
        g = np.where(valid[None, None], g, np.float32(NEG))
        tab[:, i] = g.transpose(0, 2, 1, 3)
    return tab


_CACHE = {}


def make_in_maps(inputs, S, nb):
    consts = host_constants(S)
    biasA = host_biasA(np.asarray(inputs["rel_bias"], np.float32))
    maps = []
    for b in range(nb):
        m = dict(consts)
        m["x"] = np.ascontiguousarray(inputs["x"][b, :S])
        m["positions"] = np.ascontiguousarray(inputs["positions"][b:b + 1, :S]).astype(np.int32)
        m["biasA"] = biasA
        m["lambdas"] = np.ascontiguousarray(np.asarray(inputs["lambdas"], np.float32).reshape(DEPTH, 1, 256))
        for k in ("norm_mix", "w_in", "sub_gain", "b_forget", "w_br_a", "w_br_b", "w_br_c", "w_gate", "b_gate",
                  "w_out", "norm_mlp", "w_ff1", "w_ff2", "final_norm"):
            m[k] = np.ascontiguousarray(np.asarray(inputs[k], np.float32))
        maps.append(m)
    return maps


def kernel(**inputs):
    S = inputs["x"].shape[1]
    nb = inputs["x"].shape[0]
    if "nc" not in _CACHE:
        _CACHE["nc"] = build_program(S)
    nc = _CACHE["nc"]
    maps = make_in_maps(inputs, S, nb)
    res = run_bass_kernel_spmd(nc, maps, core_ids=list(range(nb)))
    return np.stack([r["out"] for r in res.results], axis=0).astype(np.float32)
```
